# Optimizing a Trainium2 kernel written in Bass

```python
import math
import jax, jax.numpy as jnp
from jax import lax
import numpy as np


D_MODEL = 1024
BATCH = 8
SEQ = 8192
DEPTH = 1

MEM_TOKENS = 256
S5_WIDTH = D_MODEL // 4
S5_GROUP_CH = 16
S5_GROUPS = S5_WIDTH // S5_GROUP_CH
S5_STATE = 64
S5_MAX_RE = -1e-4
S5_DT_MIN = 1e-3
S5_DT_MAX = 1e-1
MLA_HEADS = 8
MLA_NOPE_DIM = 64
MLA_ROPE_DIM = 32
MLA_QK_DIM = MLA_NOPE_DIM + MLA_ROPE_DIM
MLA_V_DIM = 64
MLA_Q_RANK = D_MODEL // 4
MLA_KV_RANK = D_MODEL // 4
ROPE_THETA = 10000.0
Q_BLOCK = 128
XATTN_HEADS = 4
XATTN_HEAD_DIM = D_MODEL // XATTN_HEADS
MLP_HIDDEN = 4 * D_MODEL
LN_EPS = 1e-5
RMS_EPS = 1e-6
NEG_INF = -1e30
POS_OFFSET_MAX = 4096
DN_ALPHA = (2.0 * DEPTH) ** 0.25
DN_BETA = (8.0 * DEPTH) ** -0.25
IN_S5 = S5_WIDTH
IN_Q = MLA_Q_RANK
IN_KV = MLA_KV_RANK
IN_KR = MLA_ROPE_DIM
IN_GATE = 2 * D_MODEL
IN_WIDTH = IN_S5 + IN_Q + IN_KV + IN_KR + IN_GATE

kernel_name = "hybrid_s5_mla_gated_deepnorm_layer"


def layer_norm(x, g, b):
    xf = x.astype(jnp.float32)
    mu = jnp.mean(xf, axis=-1, keepdims=True)
    xc = xf - mu
    var = jnp.mean(xc * xc, axis=-1, keepdims=True)
    return (xc * lax.rsqrt(var + LN_EPS) * g.astype(jnp.float32) + b.astype(jnp.float32)).astype(x.dtype)


def rms_norm(x, g):
    xf = x.astype(jnp.float32)
    return (xf * lax.rsqrt(jnp.mean(xf * xf, axis=-1, keepdims=True) + RMS_EPS) * g.astype(jnp.float32)).astype(x.dtype)


def rope_tables(positions):
    inv = ROPE_THETA ** (-jnp.arange(0, MLA_ROPE_DIM, 2, dtype=jnp.float32) / MLA_ROPE_DIM)
    ang = positions.astype(jnp.float32)[..., None] * inv
    return jnp.cos(ang)[:, :, None, :], jnp.sin(ang)[:, :, None, :]


def apply_rope(x, cos, sin):
    xf = x.astype(jnp.float32)
    x1, x2 = jnp.split(xf, 2, axis=-1)
    return jnp.concatenate([x1 * cos - x2 * sin, x1 * sin + x2 * cos], axis=-1).astype(x.dtype)


def _complex_scan_op(e1, e2):
    a1r, a1i, b1r, b1i = e1
    a2r, a2i, b2r, b2i = e2
    ar = a1r * a2r - a1i * a2i
    ai = a1r * a2i + a1i * a2r
    br = a2r * b1r - a2i * b1i + b2r
    bi = a2r * b1i + a2i * b1r + b2i
    return (ar, ai, br, bi)


def s5_ssm(u, lam_re, lam_im, log_dt, b_re, b_im, c_re, c_im, d_skip):
    bsz, seq, _ = u.shape
    uf = u.astype(jnp.float32).reshape(bsz, seq, S5_GROUPS, S5_GROUP_CH)
    lr = jnp.minimum(lam_re.astype(jnp.float32), S5_MAX_RE)
    li = lam_im.astype(jnp.float32)
    dt = jnp.exp(log_dt.astype(jnp.float32))[:, None]
    mag = jnp.exp(lr * dt)
    ang = li * dt
    ab_re = mag * jnp.cos(ang)
    ab_im = mag * jnp.sin(ang)
    den = lr * lr + li * li
    nr = ab_re - 1.0
    f_re = ((nr * lr + ab_im * li) / den)[..., None]
    f_im = ((ab_im * lr - nr * li) / den)[..., None]
    br = b_re.astype(jnp.float32)
    bi = b_im.astype(jnp.float32)
    bb_re = f_re * br - f_im * bi
    bb_im = f_re * bi + f_im * br
    bu_re = jnp.einsum('bsgh,gph->bsgp', uf, bb_re)
    bu_im = jnp.einsum('bsgh,gph->bsgp', uf, bb_im)
    a_re = jnp.broadcast_to(ab_re[None, None], (1, seq, S5_GROUPS, S5_STATE))
    a_im = jnp.broadcast_to(ab_im[None, None], (1, seq, S5_GROUPS, S5_STATE))
    _, _, h_re, h_im = lax.associative_scan(_complex_scan_op, (a_re, a_im, bu_re, bu_im), axis=1)
    y = (jnp.einsum('bsgp,ghp->bsgh', h_re, c_re.astype(jnp.float32))
         - jnp.einsum('bsgp,ghp->bsgh', h_im, c_im.astype(jnp.float32)))
    y = y + d_skip.astype(jnp.float32).reshape(S5_GROUPS, S5_GROUP_CH) * uf
    return y.reshape(bsz, seq, S5_WIDTH)


def causal_block_attention(q, k, v):
    bsz, seq, heads, dqk = q.shape
    nblk = seq // Q_BLOCK
    scale = dqk ** -0.5
    qb = jnp.moveaxis(q.reshape(bsz, nblk, Q_BLOCK, heads, dqk), 1, 0)
    starts = jnp.arange(nblk, dtype=jnp.int32) * Q_BLOCK
    kpos = jnp.arange(seq, dtype=jnp.int32)

    def block(args):
        qi, start = args
        s = jnp.einsum('bqhd,bkhd->bhqk', qi, k).astype(jnp.float32) * scale
        qpos = start + jnp.arange(Q_BLOCK, dtype=jnp.int32)
        s = jnp.where(kpos[None, :] <= qpos[:, None], s, NEG_INF)
        p = jax.nn.softmax(s, axis=-1).astype(v.dtype)
        return jnp.einsum('bhqk,bkhd->bqhd', p, v)

    o = lax.map(block, (qb, starts))
    return jnp.moveaxis(o, 0, 1).reshape(bsz, seq, heads * v.shape[-1])


def hybrid_mixer(h, cos, sin, w_in, s5_lam_re, s5_lam_im, s5_log_dt, s5_b_re, s5_b_im,
                 s5_c_re, s5_c_im, s5_d, w_glu, q_norm_g, w_uq, kv_norm_g, w_ukv, w_oa, w_o):
    bsz, seq, _ = h.shape
    z = h @ w_in
    o1 = IN_S5
    o2 = o1 + IN_Q
    o3 = o2 + IN_KV
    o4 = o3 + IN_KR
    u = z[..., :o1]
    c_q = z[..., o1:o2]
    c_kv = z[..., o2:o3]
    k_r = z[..., o3:o4]
    gate = z[..., o4:]
    y = s5_ssm(u, s5_lam_re, s5_lam_im, s5_log_dt, s5_b_re, s5_b_im, s5_c_re, s5_c_im, s5_d).astype(h.dtype)
    y = jax.nn.gelu(y, approximate=False) @ w_glu
    s_out = y[..., :D_MODEL] * jax.nn.sigmoid(y[..., D_MODEL:])
    q = (rms_norm(c_q, q_norm_g) @ w_uq).reshape(bsz, seq, MLA_HEADS, MLA_QK_DIM)
    q = jnp.concatenate([q[..., :MLA_NOPE_DIM], apply_rope(q[..., MLA_NOPE_DIM:], cos, sin)], axis=-1)
    kv = (rms_norm(c_kv, kv_norm_g) @ w_ukv).reshape(bsz, seq, MLA_HEADS, MLA_NOPE_DIM + MLA_V_DIM)
    k_rope = apply_rope(k_r[:, :, None, :], cos, sin)
    k = jnp.concatenate([kv[..., :MLA_NOPE_DIM],
                         jnp.broadcast_to(k_rope, (bsz, seq, MLA_HEADS, MLA_ROPE_DIM))], axis=-1)
    v = kv[..., MLA_NOPE_DIM:]
    a_out = causal_block_attention(q, k, v) @ w_oa
    g_s = jax.nn.sigmoid(gate[..., :D_MODEL])
    g_a = jax.nn.sigmoid(gate[..., D_MODEL:])
    return (g_s * s_out + g_a * a_out) @ w_o


def memory_cross_attention(h, mem, w_xq, w_xk, w_xv, w_xo):
    bsz, seq, _ = h.shape
    m = mem.shape[1]
    q = (h @ w_xq).reshape(bsz, seq, XATTN_HEADS, XATTN_HEAD_DIM)
    k = (mem @ w_xk).reshape(bsz, m, XATTN_HEADS, XATTN_HEAD_DIM)
    v = (mem @ w_xv).reshape(bsz, m, XATTN_HEADS, XATTN_HEAD_DIM)
    s = jnp.einsum('bshd,bmhd->bhsm', q, k).astype(jnp.float32) * (XATTN_HEAD_DIM ** -0.5)
    p = jax.nn.softmax(s, axis=-1).astype(v.dtype)
    o = jnp.einsum('bhsm,bmhd->bshd', p, v).reshape(bsz, seq, D_MODEL)
    return o @ w_xo


def squared_relu_mlp(h, w_up, w_down):
    return jnp.square(jax.nn.relu(h @ w_up)) @ w_down


def setup_inputs(seed: int = 0) -> dict:
    key = jax.random.key(seed)
    ks = jax.random.split(key, 40)
    f32 = jnp.float32

    def nrm(k, shape, scale):
        return jax.random.normal(k, shape, f32) * scale

    L = DEPTH
    G, P, H = S5_GROUPS, S5_STATE, S5_GROUP_CH
    n = jnp.arange(P, dtype=f32)
    positions = (jax.random.randint(ks[2], (BATCH, 1), 0, POS_OFFSET_MAX, dtype=jnp.int32)
                 + jnp.arange(SEQ, dtype=jnp.int32)[None, :])
    return {
        "x": nrm(ks[0], (BATCH, SEQ, D_MODEL), 1.0),
        "mem": nrm(ks[1], (BATCH, MEM_TOKENS, D_MODEL), 1.0),
        "positions": positions,
        "ln_in_g": 1.0 + nrm(ks[3], (D_MODEL,), 0.02),
        "ln_in_b": nrm(ks[4], (D_MODEL,), 0.02),
        "w_in": nrm(ks[5], (L, D_MODEL, IN_WIDTH), D_MODEL ** -0.5),
        "s5_lam_re": -0.5 + nrm(ks[6], (L, G, P), 0.01),
        "s5_lam_im": math.pi * n + nrm(ks[7], (L, G, P), 0.01),
        "s5_log_dt": jax.random.uniform(ks[8], (L, G), f32, math.log(S5_DT_MIN), math.log(S5_DT_MAX)),
        "s5_b_re": nrm(ks[9], (L, G, P, H), (2.0 * H) ** -0.5),
        "s5_b_im": nrm(ks[10], (L, G, P, H), (2.0 * H) ** -0.5),
        "s5_c_re": nrm(ks[11], (L, G, H, P), P ** -0.5),
        "s5_c_im": nrm(ks[12], (L, G, H, P), P ** -0.5),
        "s5_d": nrm(ks[13], (L, S5_WIDTH), 1.0),
        "w_glu": nrm(ks[14], (L, S5_WIDTH, 2 * D_MODEL), S5_WIDTH ** -0.5),
        "q_norm_g": 1.0 + nrm(ks[15], (L, MLA_Q_RANK), 0.02),
        "w_uq": nrm(ks[16], (L, MLA_Q_RANK, MLA_HEADS * MLA_QK_DIM), MLA_Q_RANK ** -0.5),
        "kv_norm_g": 1.0 + nrm(ks[17], (L, MLA_KV_RANK), 0.02),
        "w_ukv": nrm(ks[18], (L, MLA_KV_RANK, MLA_HEADS * (MLA_NOPE_DIM + MLA_V_DIM)), MLA_KV_RANK ** -0.5),
        "w_oa": nrm(ks[19], (L, MLA_HEADS * MLA_V_DIM, D_MODEL), (MLA_HEADS * MLA_V_DIM) ** -0.5),
        "w_o": nrm(ks[20], (L, D_MODEL, D_MODEL), DN_BETA * D_MODEL ** -0.5),
        "ln1_g": 1.0 + nrm(ks[21], (L, D_MODEL), 0.02),
        "ln1_b": nrm(ks[22], (L, D_MODEL), 0.02),
        "w_xq": nrm(ks[23], (L, D_MODEL, D_MODEL), D_MODEL ** -0.5),
        "w_xk": nrm(ks[24], (L, D_MODEL, D_MODEL), D_MODEL ** -0.5),
        "w_xv": nrm(ks[25], (L, D_MODEL, D_MODEL), DN_BETA * D_MODEL ** -0.5),
        "w_xo": nrm(ks[26], (L, D_MODEL, D_MODEL), DN_BETA * D_MODEL ** -0.5),
        "ln2_g": 1.0 + nrm(ks[27], (L, D_MODEL), 0.02),
        "ln2_b": nrm(ks[28], (L, D_MODEL), 0.02),
        "w_up": nrm(ks[29], (L, D_MODEL, MLP_HIDDEN), DN_BETA * D_MODEL ** -0.5),
        "w_down": nrm(ks[30], (L, MLP_HIDDEN, D_MODEL), DN_BETA * MLP_HIDDEN ** -0.5),
        "ln3_g": 1.0 + nrm(ks[31], (L, D_MODEL), 0.02),
        "ln3_b": nrm(ks[32], (L, D_MODEL), 0.02),
    }


def reference(x, mem, positions, ln_in_g, ln_in_b, w_in, s5_lam_re, s5_lam_im, s5_log_dt,
              s5_b_re, s5_b_im, s5_c_re, s5_c_im, s5_d, w_glu, q_norm_g, w_uq, kv_norm_g, w_ukv,
              w_oa, w_o, ln1_g, ln1_b, w_xq, w_xk, w_xv, w_xo, ln2_g, ln2_b, w_up, w_down,
              ln3_g, ln3_b):
    cos, sin = rope_tables(positions)
    h = layer_norm(x, ln_in_g, ln_in_b)
    for l in range(DEPTH):
        mix = hybrid_mixer(h, cos, sin, w_in[l], s5_lam_re[l], s5_lam_im[l], s5_log_dt[l],
                           s5_b_re[l], s5_b_im[l], s5_c_re[l], s5_c_im[l], s5_d[l], w_glu[l],
                           q_norm_g[l], w_uq[l], kv_norm_g[l], w_ukv[l], w_oa[l], w_o[l])
        h = layer_norm(DN_ALPHA * h + mix, ln1_g[l], ln1_b[l])
        xa = memory_cross_attention(h, mem, w_xq[l], w_xk[l], w_xv[l], w_xo[l])
        h = layer_norm(DN_ALPHA * h + xa, ln2_g[l], ln2_b[l])
        ff = squared_relu_mlp(h, w_up[l], w_down[l])
        h = layer_norm(DN_ALPHA * h + ff, ln3_g[l], ln3_b[l])
    return h
```

```python
import contextlib
import math
import numpy as np
import ml_dtypes
import concourse.bass as bass
import concourse.mybir as mybir
from concourse.bass_utils import run_bass_kernel_spmd

F32 = mybir.dt.float32
BF16 = mybir.dt.bfloat16
I32 = mybir.dt.int32
AF = mybir.ActivationFunctionType
ALU = mybir.AluOpType

ENGS = ("pe", "act", "dve", "pool", "sp")
EPOCH = 12000
PI = math.pi
TWO_PI = 2.0 * math.pi
CW1, CW2, CW3 = 6.28125, 0.0019350051879882812, 3.019916050561733e-07
DN_ALPHA = 2.0 ** 0.25
LN_EPS = 1e-5
RMS_EPS = 1e-6


class Buf:
    __slots__ = ("name", "w", "r")

    def __init__(self, name=""):
        self.name = name
        self.w = None
        self.r = []


class Op:
    __slots__ = ("eng", "emit", "deps", "stream", "pos", "isdma", "vc", "waits", "signal", "cnt", "ep")


class Sched:
    def __init__(self, nc):
        self.nc = nc
        self.ops = []
        self.stream_ops = {e: [] for e in ENGS}

    def _record(self, eng, emit, reads, writes, dma_stream=None, extra_deps=()):
        op = Op()
        op.eng = eng
        op.emit = emit
        op.isdma = dma_stream is not None
        op.stream = dma_stream if op.isdma else eng
        op.signal = op.isdma
        deps = set(extra_deps)
        for b in reads:
            if b.w is not None:
                deps.add(b.w)
        for b in writes:
            if b.w is not None:
                pw = self.ops[b.w]
                if op.isdma:
                    skip = pw.stream == op.stream
                else:
                    skip = eng == "pe" and pw.stream == "pe"
                if not skip:
                    deps.add(b.w)
            for r in b.r:
                deps.add(r)
        idx = len(self.ops)
        for b in writes:
            b.w = idx
            b.r = []
        for b in reads:
            if b.w != idx:
                b.r.append(idx)
        op.deps = deps
        if emit is None:
            op.pos = 0
        else:
            so = self.stream_ops.setdefault(op.stream, [])
            so.append(idx)
            op.pos = len(so)
        self.ops.append(op)
        return idx

    def op(self, eng, emit, reads=(), writes=()):
        return self._record(eng, emit, reads, writes)

    def dma(self, queue, emit, reads=(), writes=(), stream=None):
        return self._record(queue, emit, reads, writes, dma_stream="dma:" + stream)

    def barrier(self):
        last = [so[-1] for so in self.stream_ops.values() if so]
        for e in ENGS:
            self._record(e, None, (), (), extra_deps=last)

    def finalize(self, final_waits=()):
        ops = self.ops
        known = {e: {} for e in ENGS}
        for op in ops:
            kn = known[op.eng]
            need = {}
            for d in op.deps:
                p = ops[d]
                if kn.get(p.stream, 0) < p.pos and need.get(p.stream, 0) < p.pos:
                    need[p.stream] = p.pos
            waits = []
            for st, pos in need.items():
                if kn.get(st, 0) >= pos:
                    continue
                p = ops[self.stream_ops[st][pos - 1]]
                p.signal = True
                waits.append(p)
                for k, v in p.vc.items():
                    if kn.get(k, 0) < v:
                        kn[k] = v
                kn[st] = pos
            op.waits = waits
            op.vc = dict(kn)
        self.final = []
        for b in final_waits:
            if b.w is not None:
                self.final.append(ops[b.w])
        cnt = {}
        keys = []
        for op in ops:
            if op.emit is None:
                op.signal = False
            if op.signal:
                ep, c = cnt.get(op.stream, (0, 0))
                c += 16 if op.isdma else 1
                if c > EPOCH:
                    ep += 1
                    c = 16 if op.isdma else 1
                cnt[op.stream] = (ep, c)
                op.ep = ep
                op.cnt = c
                if (op.stream, ep) not in keys:
                    keys.append((op.stream, ep))
        return keys

    def emit(self, block, sems):
        ops = self.ops
        per_eng = {e: [] for e in ENGS}
        for op in ops:
            per_eng[op.eng].append(op)

        def run(engh, lst, is_sp):
            for op in lst:
                for p in op.waits:
                    engh.wait_ge(sems[(p.stream, p.ep)], p.cnt)
                if op.emit is None:
                    continue
                ins = op.emit(engh)
                if op.signal:
                    ins.then_inc(sems[(op.stream, op.ep)], 16 if op.isdma else 1)
            if is_sp:
                for p in self.final:
                    engh.wait_ge(sems[(p.stream, p.ep)], p.cnt)

        @block.tensor
        def _(e):
            run(e, per_eng["pe"], False)

        @block.scalar
        def _(e):
            run(e, per_eng["act"], False)

        @block.vector
        def _(e):
            run(e, per_eng["dve"], False)

        @block.gpsimd
        def _(e):
            run(e, per_eng["pool"], False)

        @block.sync
        def _(e):
            run(e, per_eng["sp"], True)


class T:
    __slots__ = ("ap", "b")

    def __init__(self, ap, b=None):
        self.ap = ap
        self.b = b if b is not None else Buf()

    def __getitem__(self, k):
        return self.ap[k]


class MT:
    __slots__ = ("ap", "bufs")

    def __init__(self, t, n=4):
        self.ap = t.ap
        self.bufs = [Buf() for _ in range(n)]

    def blk(self, n):
        return T(self.ap[:, n, :], self.bufs[n])


_DSZ = {F32: 4, BF16: 2, I32: 4}


class Arena:
    def __init__(self, ar, cap):
        self.ar = ar
        self.cap = cap
        self.top = 0
        self.peak = 0

    def tile(self, shape, dt, name=""):
        n = 1
        for s in shape[1:]:
            n *= s
        nel = (n * _DSZ[dt] + 1) // 2
        nel = (nel + 15) // 16 * 16
        off = self.top
        self.top += nel
        self.peak = max(self.peak, self.top)
        assert self.top <= self.cap, f"arena overflow {name} {self.top}"
        v = self.ar[0:shape[0], off:off + (n * _DSZ[dt]) // 2]
        if dt != BF16:
            v = v.bitcast(dt)
        if len(shape) == 3:
            v = v.rearrange("p (a b) -> p a b", a=shape[1])
        elif len(shape) == 4:
            v = v.rearrange("p (a b c) -> p a b c", a=shape[1], b=shape[2])
        return T(v, Buf(name))


def build(S, phases="ASBCD", dbg=False):
    NT = S // 512
    NB = S // 128
    NCH = S // 8
    NBLK = S // 1024
    nc = bass.Bass("TRN2", target_bir_lowering=False)

    def din(name, shape, dt=F32):
        return nc.dram_tensor(name, shape, dt, kind="ExternalInput").ap()

    x = din("x", [S, 1024])
    mem = din("mem", [256, 1024])
    pos = din("positions", [1, S], I32)
    ln_in_g = din("ln_in_g", [1, 1024]); ln_in_b = din("ln_in_b", [1, 1024])
    w_in = din("w_in", [1024, 2848])
    lam_re = din("s5_lam_re", [16, 64]); lam_im = din("s5_lam_im", [16, 64]); log_dt = din("s5_log_dt", [1, 16])
    b_re = din("s5_b_re", [16, 64, 16]); b_im = din("s5_b_im", [16, 64, 16])
    c_re = din("s5_c_re", [16, 16, 64]); c_im = din("s5_c_im", [16, 16, 64])
    s5_d = din("s5_d", [16, 16])
    w_glu = din("w_glu", [256, 2048])
    q_norm_g = din("q_norm_g", [256, 1]); w_uq = din("w_uq", [256, 768])
    kv_norm_g = din("kv_norm_g", [256, 1]); w_ukv = din("w_ukv", [256, 1024])
    w_oa = din("w_oa", [512, 1024]); w_o = din("w_o", [1024, 1024])
    ln1_g = din("ln1_g", [1, 1024]); ln1_b = din("ln1_b", [1, 1024])
    w_xq = din("w_xq", [1024, 1024]); w_xk = din("w_xk", [1024, 1024])
    w_xv = din("w_xv", [1024, 1024]); w_xo = din("w_xo", [1024, 1024])
    ln2_g = din("ln2_g", [1, 1024]); ln2_b = din("ln2_b", [1, 1024])
    w_up = din("w_up", [1024, 4096]); w_down = din("w_down", [4096, 1024])
    ln3_g = din("ln3_g", [1, 1024]); ln3_b = din("ln3_b", [1, 1024])
    c_ident = din("c_ident", [128, 128]); c_cm = din("c_cm", [128, 128]); c_tri = din("c_tri", [128, 128])
    c_rope = din("c_rope", [128, 3]); c_kvec = din("c_kvec", [64, 16]); c_cvec = din("c_cvec", [64, 128])
    out = nc.dram_tensor("out", [S, 1024], F32, kind="ExternalOutput").ap()
    skind = "ExternalOutput" if dbg else "Internal"
    Qs = nc.dram_tensor("Qs", [8, 96, S], BF16, kind=skind).ap()
    Ks = nc.dram_tensor("Ks", [8, 96, S], BF16, kind=skind).ap()
    Vs = nc.dram_tensor("Vs", [8, 128, NB, 65], BF16, kind=skind).ap()
    Os = nc.dram_tensor("Os", [512, S], BF16, kind=skind).ap()
    Ys = nc.dram_tensor("Ys", [256, S], BF16, kind=skind).ap()
    H2 = nc.dram_tensor("H2", [S, 1024], F32, kind=skind).ap()
    H0 = nc.dram_tensor("H0", [S, 1024], F32).ap()
    H0T = nc.dram_tensor("H0T", [1024, S], BF16).ap()

    es = contextlib.ExitStack()
    CAP = 106400
    ar = es.enter_context(nc.sbuf_tensor("arena", [128, CAP], BF16))
    A = Arena(ar, CAP)
    PSB = []
    PSP = []
    for i in range(4):
        pt = es.enter_context(nc.psum_tensor(f"ps{i}", [128, 1024], F32))
        PSP.append(pt)
        PSB.append(T(pt[:, 0:512], Buf(f"ps{2 * i}")))
        PSB.append(T(pt[:, 512:1024], Buf(f"ps{2 * i + 1}")))
    S_ = Sched(nc)
    state = {"ps": 0, "dq": 0}

    def psum():
        i = state["ps"]
        state["ps"] = (i + 1) % 8
        return PSB[i]

    def bfv(ps):
        return ps.ap.bitcast(BF16)

    def bl(ts):
        o = []
        for t in ts:
            if isinstance(t, MT):
                o.extend(t.bufs)
            elif isinstance(t, T):
                o.append(t.b)
            else:
                o.append(t)
        return o

    def PE(o, lhsT, rhs, st, sp, R, W):
        S_.op("pe", lambda e: e.matmul(o, lhsT=lhsT, rhs=rhs, start=st, stop=sp), bl(R), bl(W))

    def PET(o, i, ident, R, W):
        S_.op("pe", lambda e: e.transpose(o, i, ident), bl(R), bl(W))

    def ACT(o, i, func, R, W, scale=1.0, bias=0.0, accum=None):
        if accum is None:
            S_.op("act", lambda e: e.activation(out=o, in_=i, func=func, bias=bias, scale=scale), bl(R), bl(W))
        else:
            S_.op("act", lambda e: e.activation(out=o, in_=i, func=func, bias=bias, scale=scale, accum_out=accum), bl(R), bl(W))

    def TT(eng, o, a, b, op, R, W):
        S_.op(eng, lambda e: e.tensor_tensor(out=o, in0=a, in1=b, op=op), bl(R), bl(W))

    def TS(eng, o, a, s1, s2, op0, op1, R, W):
        if op1 is None:
            S_.op(eng, lambda e: e.tensor_scalar(out=o, in0=a, scalar1=s1, scalar2=None, op0=op0), bl(R), bl(W))
        else:
            S_.op(eng, lambda e: e.tensor_scalar(out=o, in0=a, scalar1=s1, scalar2=s2, op0=op0, op1=op1), bl(R), bl(W))

    def STT(eng, o, a, s, b, op0, op1, R, W):
        eng = "dve"
        S_.op(eng, lambda e: e.scalar_tensor_tensor(out=o, in0=a, scalar=s, in1=b, op0=op0, op1=op1), bl(R), bl(W))

    def CP(eng, o, i, R, W):
        if eng == "act":
            S_.op("act", lambda e: e.activation(out=o, in_=i, func=AF.Copy), bl(R), bl(W))
        else:
            S_.op(eng, lambda e: e.tensor_copy(out=o, in_=i), bl(R), bl(W))

    def MS(eng, o, v, W):
        S_.op(eng, lambda e: e.memset(o, v), (), bl(W))

    def DMA(q, o, i, R, W, stream, nonc=False):
        if nonc:
            S_.dma(q, lambda e: e.dma_start(out=o, in_=i, allow_slow_non_contiguous=True), bl(R), bl(W), stream=stream)
        else:
            S_.dma(q, lambda e: e.dma_start(out=o, in_=i), bl(R), bl(W), stream=stream)

    def rr(eng, xt, kf, ki, R0):
        TS(eng, kf.ap, xt.ap, 1.0 / TWO_PI, 0.5, ALU.mult, ALU.add, [xt] + R0, [kf])
        CP(eng, ki.ap, kf.ap, [kf], [ki])
        CP(eng, kf.ap, ki.ap, [ki], [kf])
        for cc in (CW1, CW2, CW3):
            STT(eng, xt.ap, kf.ap, -cc, xt.ap, ALU.mult, ALU.add, [kf, xt], [xt])
        TS(eng, kf.ap, xt.ap, -PI, TWO_PI, ALU.is_lt, ALU.mult, [xt], [kf])
        TT(eng, xt.ap, xt.ap, kf.ap, ALU.add, [xt, kf], [xt])
        TS(eng, xt.ap, xt.ap, -PI, PI, ALU.max, ALU.min, [xt], [xt])

    ident_f = A.tile([128, 128], F32, "ident_f")
    ident_b = A.tile([128, 128], BF16, "ident_b")
    ones_f = A.tile([128, 128], F32, "ones_f")
    ones_b = A.tile([128, 128], BF16, "ones_b")
    tri_b = A.tile([128, 128], BF16, "tri_b")
    cm_f = A.tile([128, 128], F32, "cm_f")
    ropec = A.tile([128, 3], F32, "ropec")
    DMA("sp", ident_f.ap, c_ident, [], [ident_f], "c0")
    DMA("sp", cm_f.ap, c_cm, [], [cm_f], "c1")
    DMA("sp", ropec.ap, c_rope, [], [ropec], "c2")
    DMA("pool", ident_b.ap, c_ident, [], [ident_b], "c3")
    DMA("pool", tri_b.ap, c_tri, [], [tri_b], "c4")
    mhalf = A.tile([128, 4], F32, "mhalf")
    MS("dve", mhalf.ap, -0.5, [mhalf])
    MS("dve", ones_f.ap, 1.0, [ones_f])
    MS("dve", ones_b.ap, 1.0, [ones_b])
    base00 = A.top
    KmT = A.tile([128, 8, 256], BF16, "KmT")
    Vm = A.tile([128, 2, 1024], BF16, "Vm")
    base0 = A.top
    Ug_all = A.tile([128, 16, NCH], BF16, "Ug_all")
    Ug_b = [[Buf() for _ in range(NBLK)] for _ in range(16)]
    base_top = A.top

    def load_bc(dst, src):
        DMA("sp", dst.ap, src.partition_broadcast(128)[:, 0, :], [], [dst], "bc_" + dst.b.name)

    ln_tmp = {}

    def ln_alloc(ntmp):
        ln_tmp["st"] = [A.tile([128, 12], F32, "st") for _ in range(4)]
        ln_tmp["stb"] = [[Buf(), Buf()] for _ in range(4)]
        ln_tmp["mv"] = [A.tile([128, 2], F32, "mv") for _ in range(4)]
        ln_tmp["rs"] = A.tile([128, 4, 2], F32, "rs")

    def ln4(xs, g_bc, b_bc, hb, defer=False):
        st = ln_tmp["st"]; stb = ln_tmp["stb"]; mv = ln_tmp["mv"]; rs = ln_tmp["rs"]
        for n in range(4):
            xb = xs.blk(n)
            S_.op("dve", lambda e, n=n, xb=xb: e.bn_stats(st[n].ap[:, 0:6], xb.ap[:, 0:512]), [xb.b], [stb[n][0]])
            S_.op("dve", lambda e, n=n, xb=xb: e.bn_stats(st[n].ap[:, 6:12], xb.ap[:, 512:1024]), [xb.b], [stb[n][1]])
        for n in range(4):
            S_.op("dve", lambda e, n=n: e.bn_aggr(mv[n].ap, st[n].ap), stb[n], [mv[n].b])
        for n in range(4):
            TS("dve", rs.ap[:, n, 0:1], mv[n].ap[:, 1:2], LN_EPS, None, ALU.add, None, [mv[n]], [rs])
        TT("pool", rs.ap[:, :, 0], rs.ap[:, :, 0], mhalf.ap[:, 0:4], ALU.pow, [rs, mhalf], [rs])
        for n in range(4):
            STT("dve", rs.ap[:, n, 1:2], mv[n].ap[:, 0:1], -1.0, rs.ap[:, n, 0:1], ALU.mult, ALU.mult, [mv[n], rs], [rs])
        for n in range(4):
            xb = xs.blk(n)
            TS("dve", xb.ap, xb.ap, rs.ap[:, n, 0:1], rs.ap[:, n, 1:2], ALU.mult, ALU.add, [xb, rs], [xb])
        for n in range(4):
            xb = xs.blk(n)
            TT("dve", xb.ap, xb.ap, g_bc.ap, ALU.mult, [xb, g_bc], [xb])
        for n in range(4):
            xb = xs.blk(n)
            TT("pool" if n % 2 == 0 else "dve", xb.ap, xb.ap, b_bc.ap, ALU.add, [xb, b_bc], [xb])

        def copies():
            if hb is not None:
                for n in range(4):
                    xb = xs.blk(n)
                    hbb = hb.blk(n)
                    CP("act", hbb.ap, xb.ap, [xb], [hbb])
        if defer:
            return copies
        copies()

    def transpose_blocks(hb, hT):
        for c in range(8):
            ps = psum()
            pv = bfv(ps)
            for n in range(4):
                PET(pv[:, n * 128:(n + 1) * 128], hb.ap[:, n, c * 128:(c + 1) * 128], ident_b.ap, [hb, ident_b], [ps])
            CP("act" if c % 2 else "dve", hT.ap[:, c, :], pv[:, 0:512], [ps], [hT.bufs[c]] if isinstance(hT, MT) else [hT])

    def wload(dst_ap, src_ap, W, stream, nonc=False):
        DMA("pool", dst_ap, src_ap, [], W, stream, nonc)

    final_bufs = []

    if "A" in phases:
        A.top = base_top
        WA = A.tile([128, 8, 960], BF16, "WA")
        Wuq = A.tile([128, 2, 768], BF16, "Wuq")
        Wuqs = A.tile([128, 2, 768], BF16, "Wuqs")
        Wk = A.tile([128, 2, 512], BF16, "Wk")
        Wv = A.tile([128, 2, 512], BF16, "Wv")
        qg = A.tile([128, 2], F32, "qg"); kvg = A.tile([128, 2], F32, "kvg")
        g_in = A.tile([128, 1024], F32, "g_in"); b_in = A.tile([128, 1024], F32, "b_in")
        w_in_r = w_in.rearrange("(kc p) n -> p kc n", p=128)
        MS("pool", WA.ap[:, :, 768:960], 0.0, [WA])
        MS("pool", Wuqs.ap, 0.0, [Wuqs])
        wload(WA.ap[:, :, 0:768], w_in_r[:, :, 0:768], [WA], "wA")
        wload(WA.ap[:, :, 832:864], w_in_r[:, :, 768:800], [WA], "wA")
        wload(WA.ap[:, :, 928:944], w_in_r[:, :, 784:800], [WA], "wA")
        wload(WA.ap[:, :, 944:960], w_in_r[:, :, 768:784], [WA], "wA")
        w_uq_r = w_uq.rearrange("(kc p) n -> p kc n", p=128)
        wload(Wuq.ap, w_uq_r, [Wuq], "wuq")
        for h in range(8):
            wload(Wuqs.ap[:, :, h * 96 + 64:h * 96 + 80], w_uq_r[:, :, h * 96 + 80:h * 96 + 96], [Wuqs], "wuqs")
            wload(Wuqs.ap[:, :, h * 96 + 80:h * 96 + 96], w_uq_r[:, :, h * 96 + 64:h * 96 + 80], [Wuqs], "wuqs")
        w_ukv_r = w_ukv.rearrange("(kc p) n -> p kc n", p=128)
        for h in range(8):
            wload(Wk.ap[:, :, h * 64:(h + 1) * 64], w_ukv_r[:, :, h * 128:h * 128 + 64], [Wk], "wk")
            wload(Wv.ap[:, :, h * 64:(h + 1) * 64], w_ukv_r[:, :, h * 128 + 64:h * 128 + 128], [Wv], "wv")
        DMA("sp", qg.ap, q_norm_g.rearrange("(c p) o -> p (c o)", p=128), [], [qg], "qg", nonc=True)
        DMA("sp", kvg.ap, kv_norm_g.rearrange("(c p) o -> p (c o)", p=128), [], [kvg], "kvg", nonc=True)
        load_bc(g_in, ln_in_g); load_bc(b_in, ln_in_b)
        ln_alloc(2)
        xt = [MT(A.tile([128, 4, 1024], F32, "xt")) for _ in range(2)]
        hb = MT(A.tile([128, 4, 1024], BF16, "hb"))
        hT = MT(A.tile([128, 8, 512], BF16, "hT"), 8)
        cT = [A.tile([128, 512], F32, "cT") for _ in range(4)]
        sq = [A.tile([128, 512], F32, "sq") for _ in range(4)]
        epsc = A.tile([128, 1], F32, "epsc")
        MS("dve", epsc.ap, RMS_EPS, [epsc])

        rstd = [A.tile([128, 512], F32, "rstd") for _ in range(2)]
        cn = [A.tile([128, 2, 512], BF16, "cn") for _ in range(2)]
        Qt = MT(A.tile([128, 8, 512], BF16, "Qt"), 16)
        Kt = MT(A.tile([128, 8, 512], BF16, "Kt"), 9)
        Vt = MT(A.tile([128, 4, 8, 65], BF16, "Vt"), 4)
        Uc = MT(A.tile([128, 16, 8, 16], BF16, "Uc"), 4)
        posi = A.tile([128, 512], I32, "posi")
        angS = A.tile([128, 512], F32, "angS"); angC = A.tile([128, 512], F32, "angC")
        rkf = A.tile([128, 512], F32, "rkf"); rki = A.tile([128, 512], I32, "rki")
        rkf2 = A.tile([128, 512], F32, "rkf2"); rki2 = A.tile([128, 512], I32, "rki2")
        ropeCs = [A.tile([128, 512], F32, "ropeC") for _ in range(2)]
        ropeSs = [A.tile([128, 512], F32, "ropeS") for _ in range(2)]
        t1 = [A.tile([128, 512], F32, "t1") for _ in range(2)]
        t2 = [A.tile([128, 512], F32, "t2") for _ in range(2)]
        krf = A.tile([128, 512], F32, "krf")
        MS("pool", Vt.ap, 1.0, [Vt])
        R = slice(64, 96)

        def sub(tl):
            return T(tl.ap[R, :], tl.b)

        def ld_x(t):
            DMA("sp", xt[t % 2].ap, x[t * 512:(t + 1) * 512, :].rearrange("(n p) d -> p n d", p=128), [], [xt[t % 2]], f"x{t % 2}")

        def rope_tab(t):
            c0 = t * 512
            ropeC = ropeCs[t % 2]; ropeS = ropeSs[t % 2]
            DMA("sp", posi.ap[R, :], pos[:, c0:c0 + 512].partition_broadcast(32)[:, 0, :], [], [posi], "posi")
            CP("dve", angS.ap[R, :], posi.ap[R, :], [posi], [angS])
            TS("dve", angS.ap[R, :], angS.ap[R, :], ropec.ap[R, 0:1], None, ALU.mult, None, [angS, ropec], [angS])
            rr("dve", sub(angS), sub(rkf), sub(rki), [])
            ACT(ropeS.ap[R, :], angS.ap[R, :], AF.Sin, [angS, ropec], [ropeS], scale=ropec.ap[R, 1:2])
            STT("dve", angC.ap[R, :], angS.ap[R, :], -1.0, angS.ap[R, :], ALU.mult, ALU.max, [angS], [angC])
            ACT(ropeC.ap[R, :], angC.ap[R, :], AF.Sin, [angC, ropec], [ropeC], scale=-1.0, bias=ropec.ap[R, 2:3])

        ld_x(0)
        rope_tab(0)
        ln4(xt[0], g_in, b_in, hb)
        for t in range(NT):
            c0 = t * 512
            xs = xt[t % 2]
            ropeC = ropeCs[t % 2]; ropeS = ropeSs[t % 2]
            if t + 1 < NT:
                ld_x(t + 1)
            transpose_blocks(hb, hT)
            DMA("sp", H0[c0:c0 + 512, :].rearrange("(n p) d -> p n d", p=128), xs.ap, [xs], [], "stH0")
            DMA("sp", H0T[:, c0:c0 + 512].rearrange("(k p) s -> p k s", p=128), hT.ap, [hT], [], "stH0T")
            cps = None
            for i in range(4):
                ps = psum()
                for kc in range(8):
                    PE(ps.ap, WA.ap[:, kc, 256 + i * 128:256 + (i + 1) * 128], hT.ap[:, kc, :], kc == 0, kc == 7, [WA, hT], [ps])
                CP("act", cT[i].ap, ps.ap, [ps], [cT[i]])
            psms = []
            for j in range(2):
                psm = psum()
                psms.append(psm)
                for i in range(2):
                    sqt = sq[(2 * j + i) % len(sq)]
                    ACT(sqt.ap, cT[2 * j + i].ap, AF.Square, [cT[2 * j + i]], [sqt])
                    PE(psm.ap, ones_f.ap, sqt.ap, i == 0, i == 1, [ones_f, sqt], [psm])
            for j in range(2):
                ACT(rstd[j].ap, psms[j].ap, AF.Ln, [psms[j], epsc], [rstd[j]], scale=1.0 / 256.0, bias=epsc.ap[:, 0:1])
            for j in range(2):
                ACT(rstd[j].ap, rstd[j].ap, AF.Exp, [rstd[j]], [rstd[j]], scale=-0.5)
            for j in range(2):
                gcol = qg if j == 0 else kvg
                for i in range(2):
                    STT("dve", cn[j].ap[:, i, :], cT[2 * j + i].ap, gcol.ap[:, i:i + 1], rstd[j].ap,
                        ALU.mult, ALU.mult, [cT[2 * j + i], gcol, rstd[j]], [cn[j]])
            pa = psum(); pb = psum()
            for kc in range(8):
                PE(pa.ap[0:96, :], WA.ap[:, kc, 768:864], hT.ap[:, kc, :], kc == 0, kc == 7, [WA, hT], [pa])
            for kc in range(8):
                PE(pb.ap[0:96, :], WA.ap[:, kc, 864:960], hT.ap[:, kc, :], kc == 0, kc == 7, [WA, hT], [pb])
            TT("dve", t1[0].ap[R, :], pa.ap[R, :], ropeC.ap[R, :], ALU.mult, [pa, ropeC], [t1[0]])
            TT("dve", t2[0].ap[R, :], pb.ap[R, :], ropeS.ap[R, :], ALU.mult, [pb, ropeS], [t2[0]])
            TT("pool", krf.ap[R, :], t1[0].ap[R, :], t2[0].ap[R, :], ALU.add, [t1[0], t2[0]], [krf])
            CP("act", Kt.ap[R, :, :], krf.ap[R, :].unsqueeze(1).to_broadcast([32, 8, 512]), [krf], [Kt.bufs[8]])
            for tp in range(4):
                ps = psum()
                for tl in range(2):
                    tau = tp * 2 + tl
                    for kc in range(8):
                        PE(ps.ap[0:64, tl * 256:(tl + 1) * 256], hT.ap[:, kc, tau:512:8], WA.ap[:, kc, 0:256], kc == 0, kc == 7, [hT, WA], [ps])
                CP("act" if tp % 2 else "dve", Uc.ap[0:64, :, tp * 2:tp * 2 + 2, :], ps.ap[0:64, :].rearrange("p (t g h) -> p g t h", t=2, g=16), [ps], [Uc.bufs[tp]])
            blk = c0 // 1024
            for gh in range(2):
                ps = psum()
                pv = bfv(ps)
                for g8 in range(8):
                    g = gh * 8 + g8
                    PET(pv[:, g8 * 64:(g8 + 1) * 64], Uc.ap[0:64, g, :, :].rearrange("p a b -> p (a b)"), ident_b.ap[0:64, 0:64], [Uc, ident_b], [ps])
                CP("act" if gh else "dve", Ug_all.ap[:, gh * 8:(gh + 1) * 8, t * 64:(t + 1) * 64],
                   pv[:, 0:512].rearrange("p (g c) -> p g c", g=8), [ps], [Ug_b[gh * 8 + g8][blk] for g8 in range(8)])
            if t + 1 < NT:
                cps = ln4(xt[(t + 1) % 2], g_in, b_in, hb, defer=True)
            for h in range(8):
                pa = psum(); pb = psum()
                for kc in range(2):
                    PE(pa.ap[0:96, :], Wuq.ap[:, kc, h * 96:(h + 1) * 96], cn[0].ap[:, kc, :], kc == 0, kc == 1, [Wuq, cn[0]], [pa])
                for kc in range(2):
                    PE(pb.ap[0:96, :], Wuqs.ap[:, kc, h * 96:(h + 1) * 96], cn[0].ap[:, kc, :], kc == 0, kc == 1, [Wuqs, cn[0]], [pb])
                CP("act", Qt.ap[0:64, h, :], pa.ap[0:64, :], [pa], [Qt.bufs[h]])
                TT("dve", t1[h % 2].ap[R, :], pa.ap[R, :], ropeC.ap[R, :], ALU.mult, [pa, ropeC], [t1[h % 2]])
                TT("dve", t2[h % 2].ap[R, :], pb.ap[R, :], ropeS.ap[R, :], ALU.mult, [pb, ropeS], [t2[h % 2]])
                TT("pool", Qt.ap[R, h, :], t1[h % 2].ap[R, :], t2[h % 2].ap[R, :], ALU.add, [t1[h % 2], t2[h % 2]], [Qt.bufs[8 + h]])
            for h in range(8):
                ps = psum()
                for kc in range(2):
                    PE(ps.ap[0:64, :], Wk.ap[:, kc, h * 64:(h + 1) * 64], cn[1].ap[:, kc, :], kc == 0, kc == 1, [Wk, cn[1]], [ps])
                CP("act" if h % 2 else "dve", Kt.ap[0:64, h, :], ps.ap[0:64, :], [ps], [Kt.bufs[h]])
            for n in range(4):
                ps = psum()
                for kc in range(2):
                    PE(ps.ap, cn[1].ap[:, kc, n * 128:(n + 1) * 128], Wv.ap[:, kc, :], kc == 0, kc == 1, [cn[1], Wv], [ps])
                CP("act" if n % 2 else "dve", Vt.ap[:, n, :, 0:64], ps.ap.rearrange("p (h c) -> p h c", h=8), [ps], [Vt.bufs[n]])
            if cps is not None:
                cps()
            if t + 1 < NT:
                rope_tab(t + 1)
            DMA("sp", Qs[:, :, c0:c0 + 512].rearrange("h r s -> r h s"), Qt.ap[0:96, :, :], [Qt], [], "stQ")
            DMA("sp", Ks[:, :, c0:c0 + 512].rearrange("h r s -> r h s"), Kt.ap[0:96, :, :], [Kt], [], "stK")
            for h in range(8):
                DMA("sp", Vs[h, :, t * 4:(t + 1) * 4, :], Vt.ap[:, :, h, :], [Vt], [], "stV")
        S_.barrier()

    if "S" in phases:
        A.top = base_top
        P64 = slice(0, 64)
        BmR = A.tile([128, 16, 64], BF16, "BmR"); BmI = A.tile([128, 16, 64], BF16, "BmI")
        Tg = A.tile([128, 16, 128], BF16, "Tg")
        CmR = A.tile([64, 16, 128], BF16, "CmR"); CmI = A.tile([64, 16, 128], BF16, "CmI")
        ER = A.tile([64, 16, 128], F32, "ER"); EI = A.tile([64, 16, 128], F32, "EI")
        Rt = A.tile([64, 16, 128], F32, "Rt")
        c1s = A.tile([64, 16], F32, "c1s"); s1s = A.tile([64, 16], F32, "s1s")
        dcols = A.tile([128, 16], F32, "dcols")
        s_top = A.top
        lr = A.tile([64, 16], F32, "lr"); li = A.tile([64, 16], F32, "li"); dtb = A.tile([64, 16], F32, "dtb")
        aa = A.tile([64, 16], F32, "aa"); ang = A.tile([64, 16], F32, "ang")
        kvec = A.tile([64, 16], F32, "kvec"); cvec = A.tile([64, 128], F32, "cvec")
        DMA("sp", lr.ap, lam_re.rearrange("g p -> p g"), [], [lr], "s5a", nonc=True)
        DMA("sp", li.ap, lam_im.rearrange("g p -> p g"), [], [li], "s5b", nonc=True)
        DMA("sp", dtb.ap, log_dt.partition_broadcast(64)[:, 0, :], [], [dtb], "s5c")
        DMA("sp", kvec.ap, c_kvec, [], [kvec], "s5d")
        DMA("sp", cvec.ap, c_cvec, [], [cvec], "s5e")
        for s in range(8):
            DMA("sp", dcols.ap[s * 16:(s + 1) * 16, :], s5_d.rearrange("g h -> h g"), [], [dcols], "s5f", nonc=True)
        Bre = A.tile([64, 16, 16], F32, "Bre"); Bim = A.tile([64, 16, 16], F32, "Bim")
        Cre = A.tile([64, 16, 16], F32, "Cre"); Cim = A.tile([64, 16, 16], F32, "Cim")
        DMA("sp", Bre.ap, b_re.rearrange("g p h -> p g h"), [], [Bre], "s5g", nonc=True)
        DMA("sp", Bim.ap, b_im.rearrange("g p h -> p g h"), [], [Bim], "s5h", nonc=True)
        for g in range(16):
            DMA("sp", Cre.ap[:, g, :], c_re[g].rearrange("h p -> p h"), [], [Cre], "s5i", nonc=True)
            DMA("sp", Cim.ap[:, g, :], c_im[g].rearrange("h p -> p h"), [], [Cim], "s5j", nonc=True)
        TS("dve", lr.ap, lr.ap, -1e-4, None, ALU.min, None, [lr], [lr])
        ACT(dtb.ap, dtb.ap, AF.Exp, [dtb], [dtb])
        TT("dve", aa.ap, lr.ap, dtb.ap, ALU.mult, [lr, dtb], [aa])
        TT("dve", ang.ap, li.ap, dtb.ap, ALU.mult, [li, dtb], [ang])
        pwm = A.tile([64, 16, 16], F32, "pwm"); pwa = A.tile([64, 16, 16], F32, "pwa"); pwb = A.tile([64, 16, 16], F32, "pwb")
        pwr = A.tile([64, 16, 16], F32, "pwr"); pwi = A.tile([64, 16, 16], F32, "pwi")
        kfs = A.tile([64, 2048], F32, "kfs"); kis = A.tile([64, 2048], I32, "kis")

        def bc_g(tl):
            return tl.ap.unsqueeze(2).to_broadcast([64, 16, 16])

        def bc_k(tl, n=16):
            return tl.ap.unsqueeze(1).to_broadcast([64, 16, n])

        TT("dve", pwm.ap, bc_g(aa), bc_k(kvec), ALU.mult, [aa, kvec], [pwm])
        ACT(pwm.ap, pwm.ap, AF.Exp, [pwm], [pwm])
        TT("dve", pwa.ap, bc_g(ang), bc_k(kvec), ALU.mult, [ang, kvec], [pwa])
        TS("dve", pwb.ap, pwa.ap, PI / 2, None, ALU.add, None, [pwa], [pwb])

        def flat(tl, n):
            return T(tl.ap.rearrange("p a b -> p (a b)"), tl.b)

        kf256 = T(kfs.ap[:, 0:256], kfs.b); ki256 = T(kis.ap[:, 0:256], kis.b)
        rr("dve", flat(pwa, 256), kf256, ki256, [])
        rr("dve", flat(pwb, 256), kf256, ki256, [])
        ACT(pwa.ap, pwa.ap, AF.Sin, [pwa], [pwa])
        ACT(pwb.ap, pwb.ap, AF.Sin, [pwb], [pwb])
        TT("dve", pwr.ap, pwm.ap, pwb.ap, ALU.mult, [pwm, pwb], [pwr])
        TT("dve", pwi.ap, pwm.ap, pwa.ap, ALU.mult, [pwm, pwa], [pwi])
        den = A.tile([64, 16], F32, "den"); nr = A.tile([64, 16], F32, "nr")
        fr = A.tile([64, 16], F32, "fr"); fi = A.tile([64, 16], F32, "fi"); tq = A.tile([64, 16], F32, "tq")
        lbr = pwr.ap[:, :, 8]; lbi = pwi.ap[:, :, 8]
        TT("dve", den.ap, lr.ap, lr.ap, ALU.mult, [lr], [den])
        TT("dve", tq.ap, li.ap, li.ap, ALU.mult, [li], [tq])
        TT("dve", den.ap, den.ap, tq.ap, ALU.add, [den, tq], [den])
        S_.op("dve", lambda e: e.reciprocal(den.ap, den.ap), [den.b], [den.b])
        TS("dve", nr.ap, lbr, -1.0, None, ALU.add, None, [pwr], [nr])
        TT("dve", fr.ap, nr.ap, lr.ap, ALU.mult, [nr, lr], [fr])
        TT("dve", tq.ap, lbi, li.ap, ALU.mult, [pwi, li], [tq])
        TT("dve", fr.ap, fr.ap, tq.ap, ALU.add, [fr, tq], [fr])
        TT("dve", fr.ap, fr.ap, den.ap, ALU.mult, [fr, den], [fr])
        TT("dve", fi.ap, lbi, lr.ap, ALU.mult, [pwi, lr], [fi])
        TT("dve", tq.ap, nr.ap, li.ap, ALU.mult, [nr, li], [tq])
        TT("dve", fi.ap, fi.ap, tq.ap, ALU.subtract, [fi, tq], [fi])
        TT("dve", fi.ap, fi.ap, den.ap, ALU.mult, [fi, den], [fi])
        Bbr = A.tile([64, 16, 16], F32, "Bbr"); Bbi = A.tile([64, 16, 16], F32, "Bbi"); tb = A.tile([64, 16, 16], F32, "tb")
        TT("dve", Bbr.ap, Bre.ap, bc_g(fr), ALU.mult, [Bre, fr], [Bbr])
        TT("dve", tb.ap, Bim.ap, bc_g(fi), ALU.mult, [Bim, fi], [tb])
        TT("dve", Bbr.ap, Bbr.ap, tb.ap, ALU.subtract, [Bbr, tb], [Bbr])
        TT("dve", Bbi.ap, Bim.ap, bc_g(fr), ALU.mult, [Bim, fr], [Bbi])
        TT("dve", tb.ap, Bre.ap, bc_g(fi), ALU.mult, [Bre, fi], [tb])
        TT("dve", Bbi.ap, Bbi.ap, tb.ap, ALU.add, [Bbi, tb], [Bbi])
        XPr = A.tile([64, 16, 8, 16], F32, "XPr"); XPi = A.tile([64, 16, 8, 16], F32, "XPi")
        CPr = A.tile([64, 16, 16, 16], F32, "CPr"); CPn = A.tile([64, 16, 16, 16], F32, "CPn")
        tx = A.tile([64, 16, 16, 16], F32, "tx")

        def b4(tl, n):
            return tl.ap.unsqueeze(2).to_broadcast([64, 16, n, 16])

        for s in range(8):
            pr = pwr.ap[:, :, 14 - s:15 - s].to_broadcast([64, 16, 16])
            pi_ = pwi.ap[:, :, 14 - s:15 - s].to_broadcast([64, 16, 16])
            TT("dve", XPr.ap[:, :, s, :], Bbr.ap, pr, ALU.mult, [Bbr, pwr], [XPr])
            TT("pool", tb.ap, Bbi.ap, pi_, ALU.mult, [Bbi, pwi], [tb])
            TT("dve", XPr.ap[:, :, s, :], XPr.ap[:, :, s, :], tb.ap, ALU.subtract, [XPr, tb], [XPr])
            TT("dve", XPi.ap[:, :, s, :], Bbr.ap, pi_, ALU.mult, [Bbr, pwi], [XPi])
            TT("pool", tb.ap, Bbi.ap, pr, ALU.mult, [Bbi, pwr], [tb])
            TT("dve", XPi.ap[:, :, s, :], XPi.ap[:, :, s, :], tb.ap, ALU.add, [XPi, tb], [XPi])
        pr4 = pwr.ap.unsqueeze(3).to_broadcast([64, 16, 16, 16])
        pi4 = pwi.ap.unsqueeze(3).to_broadcast([64, 16, 16, 16])
        TT("dve", CPr.ap, b4(Cre, 16), pr4, ALU.mult, [Cre, pwr], [CPr])
        TT("dve", tx.ap, b4(Cim, 16), pi4, ALU.mult, [Cim, pwi], [tx])
        TT("dve", CPr.ap, CPr.ap, tx.ap, ALU.subtract, [CPr, tx], [CPr])
        TT("dve", CPn.ap, b4(Cre, 16), pi4, ALU.mult, [Cre, pwi], [CPn])
        TT("dve", tx.ap, b4(Cim, 16), pr4, ALU.mult, [Cim, pwr], [tx])
        STT("dve", CPn.ap, CPn.ap, -1.0, tx.ap, ALU.mult, ALU.subtract, [CPn, tx], [CPn])
        for g in range(16):
            CP("act", CmR.ap[:, g, :], CPr.ap[:, g, 8:16, :].rearrange("p a b -> p (a b)"), [CPr], [CmR])
            CP("pool", CmI.ap[:, g, :], CPn.ap[:, g, 8:16, :].rearrange("p a b -> p (a b)"), [CPn], [CmI])
        ttmp = A.tile([128, 128], F32, "ttmp")
        for g in range(16):
            xr = XPr.ap[:, g, :, :].rearrange("p a b -> p (a b)")
            xi = XPi.ap[:, g, :, :].rearrange("p a b -> p (a b)")
            ps = psum()
            PE(ps.ap[:, 0:64], xr, ident_f.ap[0:64, 0:64], True, True, [XPr, ident_f], [ps])
            PE(ps.ap[:, 64:128], xi, ident_f.ap[0:64, 0:64], True, True, [XPi, ident_f], [ps])
            CP("act", BmR.ap[:, g, :], ps.ap[:, 0:64], [ps], [BmR])
            CP("act", BmI.ap[:, g, :], ps.ap[:, 64:128], [ps], [BmI])
            ps2 = psum()
            yr = CPr.ap[:, g, 0:8, :].rearrange("p a b -> p (a b)")
            yn = CPn.ap[:, g, 0:8, :].rearrange("p a b -> p (a b)")
            PE(ps2.ap[:, 0:128], xr, yr, True, False, [XPr, CPr], [ps2])
            PE(ps2.ap[:, 0:128], xi, yn, False, True, [XPi, CPn], [ps2])
            TT("dve", ttmp.ap, ps2.ap[:, 0:128], cm_f.ap, ALU.mult, [ps2, cm_f], [ttmp])
            STT("dve", Tg.ap[:, g, :], ident_f.ap, dcols.ap[:, g:g + 1], ttmp.ap, ALU.mult, ALU.add, [ident_f, dcols, ttmp], [Tg])
        th = A.tile([64, 16], F32, "th"); r8 = A.tile([64, 16], F32, "r8")
        TS("dve", th.ap, ang.ap, 8.0, None, ALU.mult, None, [ang], [th])
        rr("dve", th, T(kfs.ap[:, 0:16], kfs.b), T(kis.ap[:, 0:16], kis.b), [])
        TS("dve", r8.ap, aa.ap, 8.0, None, ALU.mult, None, [aa], [r8])
        ACT(r8.ap, r8.ap, AF.Exp, [r8], [r8])
        CP("dve", Rt.ap, r8.ap.unsqueeze(2).to_broadcast([64, 16, 128]), [r8], [Rt])
        TT("dve", EI.ap, th.ap.unsqueeze(2).to_broadcast([64, 16, 128]), cvec.ap.unsqueeze(1).to_broadcast([64, 16, 128]),
           ALU.mult, [th, cvec], [EI])
        TS("dve", ER.ap, EI.ap, PI / 2, None, ALU.add, None, [EI], [ER])
        rr("dve", flat(EI, 2048), kfs, kis, [])
        rr("dve", flat(ER, 2048), kfs, kis, [])
        ACT(EI.ap, EI.ap, AF.Sin, [EI], [EI])
        ACT(ER.ap, ER.ap, AF.Sin, [ER], [ER])
        CP("dve", c1s.ap, ER.ap[:, :, 1], [ER], [c1s])
        CP("dve", s1s.ap, EI.ap[:, :, 1], [EI], [s1s])
        S_.barrier()
        A.top = s_top
        Hp = [[A.tile([64, 8, 129], BF16, "HpR") for _ in range(2)], [A.tile([64, 8, 129], BF16, "HpI") for _ in range(2)]]
        Gi = [[A.tile([64, 8], F32, "Gi") for _ in range(2)] for _ in range(2)]
        Hl = [[A.tile([64, 8], F32, "Hl") for _ in range(2)] for _ in range(2)]
        EB = [{k: A.tile([64, 8, 128], F32, k) for k in ("e1", "e2", "e3", "e4", "BpR", "BpI", "GR", "GI")} for _ in range(2)]
        Ygs = [A.tile([128, 8, 128], BF16, "Yg") for _ in range(2)]
        Yc = A.tile([128, 8, 256], BF16, "Yc")
        yTb = A.tile([128, 2, 1024], BF16, "yTb")
        hls = [[A.tile([64, 8], F32, "hl") for _ in range(2)] for _ in range(2)]
        Wxk = A.tile([128, 8, 1024], BF16, "Wxk"); Wxv = A.tile([128, 8, 1024], BF16, "Wxv")
        memb = A.tile([128, 2, 1024], BF16, "memb"); memT = A.tile([128, 8, 256], BF16, "memT")
        wload(Wxk.ap, w_xk.rearrange("(kc p) n -> p kc n", p=128), [Wxk], "wxk")
        wload(Wxv.ap, w_xv.rearrange("(kc p) n -> p kc n", p=128), [Wxv], "wxv")
        wload(memb.ap, mem.rearrange("(n p) d -> p n d", p=128), [memb], "memb")

        def mem_kv():
            for c in range(8):
                ps = psum(); pv = bfv(ps)
                for n in range(2):
                    PET(pv[:, n * 128:(n + 1) * 128], memb.ap[:, n, c * 128:(c + 1) * 128], ident_b.ap, [memb, ident_b], [ps])
                CP("act" if c % 2 else "dve", memT.ap[:, c, :], pv[:, 0:256], [ps], [memT])
            for c in range(8):
                ps = psum()
                for kc in range(8):
                    PE(ps.ap[:, 0:256], Wxk.ap[:, kc, c * 128:(c + 1) * 128], memT.ap[:, kc, :], kc == 0, kc == 7, [Wxk, memT], [ps])
                CP("act" if c % 2 else "dve", KmT.ap[:, c, :], ps.ap[:, 0:256], [ps], [KmT])
            for mb in range(2):
                for hf in range(2):
                    ps = psum()
                    for kc in range(8):
                        PE(ps.ap, memT.ap[:, kc, mb * 128:(mb + 1) * 128], Wxv.ap[:, kc, hf * 512:(hf + 1) * 512], kc == 0, kc == 7, [memT, Wxv], [ps])
                    CP("act" if hf else "dve", Vm.ap[:, mb, hf * 512:(hf + 1) * 512], ps.ap, [ps], [Vm])

        for gh in range(2):
            for ri in range(2):
                MS("dve", Gi[gh][ri].ap, 0.0, [Gi[gh][ri]])
                MS("pool", Hp[ri][gh].ap, 0.0, [Hp[ri][gh]])

        GRb = [[[Buf() for _ in range(8)] for _ in range(2)] for _ in range(2)]
        Ehb = [{k: [Buf(), Buf()] for k in ("e1", "e2", "e3", "e4")} for _ in range(2)]

        def s_stage1(blk, gh):
            cb = blk * 128
            B_ = EB[gh]
            pR = [psum(), psum()]; pI = [psum(), psum()]
            for g8 in range(8):
                g = gh * 8 + g8
                u_ap = Ug_all.ap[:, g, cb:cb + 128]
                PE(pR[g8 // 4].ap[0:64, (g8 % 4) * 128:(g8 % 4 + 1) * 128], BmR.ap[:, g, :], u_ap, True, True, [BmR, Ug_b[g][blk]], [pR[g8 // 4]])
                PE(pI[g8 // 4].ap[0:64, (g8 % 4) * 128:(g8 % 4 + 1) * 128], BmI.ap[:, g, :], u_ap, True, True, [BmI, Ug_b[g][blk]], [pI[g8 // 4]])
            for hf in range(2):
                hs = slice(hf * 4, hf * 4 + 4)
                gsl = slice(gh * 8 + hf * 4, gh * 8 + hf * 4 + 4)
                sr = pR[hf].ap[0:64, :].rearrange("p (a b) -> p a b", a=4)
                si = pI[hf].ap[0:64, :].rearrange("p (a b) -> p a b", a=4)
                TT("dve", B_["e1"].ap[:, hs, :], sr, ER.ap[:, gsl, :], ALU.mult, [pR[hf], ER], [Ehb[gh]["e1"][hf]])
                TT("dve", B_["e2"].ap[:, hs, :], si, EI.ap[:, gsl, :], ALU.mult, [pI[hf], EI], [Ehb[gh]["e2"][hf]])
                TT("dve", B_["e3"].ap[:, hs, :], si, ER.ap[:, gsl, :], ALU.mult, [pI[hf], ER], [Ehb[gh]["e3"][hf]])
                TT("dve", B_["e4"].ap[:, hs, :], sr, EI.ap[:, gsl, :], ALU.mult, [pR[hf], EI], [Ehb[gh]["e4"][hf]])
            TT("pool", B_["BpR"].ap, B_["e1"].ap, B_["e2"].ap, ALU.add, Ehb[gh]["e1"] + Ehb[gh]["e2"] + [B_["e1"], B_["e2"]], [B_["BpR"]])
            TT("pool", B_["BpI"].ap, B_["e3"].ap, B_["e4"].ap, ALU.subtract, Ehb[gh]["e3"] + Ehb[gh]["e4"] + [B_["e3"], B_["e4"]], [B_["BpI"]])

        def s_stage2(blk, gh):
            B_ = EB[gh]
            gs = slice(gh * 8, gh * 8 + 8)
            hpr = Hp[0][gh]; hpi = Hp[1][gh]
            GR = B_["GR"]; GI = B_["GI"]; HR = B_["e1"]; HI = B_["e3"]
            for g8 in range(8):
                g = gh * 8 + g8
                S_.op("dve", lambda e, g8=g8, g=g: e.tensor_tensor_scan(out=GR.ap[:, g8, :], data0=Rt.ap[:, g, :], data1=B_["BpR"].ap[:, g8, :],
                      initial=Gi[gh][0].ap[:, g8:g8 + 1], op0=ALU.mult, op1=ALU.add), [Rt.b, B_["BpR"].b, Gi[gh][0].b], [GRb[gh][0][g8]])
                S_.op("dve", lambda e, g8=g8, g=g: e.tensor_tensor_scan(out=GI.ap[:, g8, :], data0=Rt.ap[:, g, :], data1=B_["BpI"].ap[:, g8, :],
                      initial=Gi[gh][1].ap[:, g8:g8 + 1], op0=ALU.mult, op1=ALU.add), [Rt.b, B_["BpI"].b, Gi[gh][1].b], [GRb[gh][1][g8]])
            gr_ = GRb[gh][0]; gi_ = GRb[gh][1]
            TT("dve", B_["e1"].ap, GR.ap, ER.ap[:, gs, :], ALU.mult, gr_ + [ER], [B_["e1"]] + Ehb[gh]["e1"])
            TT("pool", B_["e2"].ap, GI.ap, EI.ap[:, gs, :], ALU.mult, gi_ + [EI], [B_["e2"]] + Ehb[gh]["e2"])
            TT("dve", B_["e3"].ap, GR.ap, EI.ap[:, gs, :], ALU.mult, gr_ + [EI], [B_["e3"]] + Ehb[gh]["e3"])
            TT("pool", B_["e4"].ap, GI.ap, ER.ap[:, gs, :], ALU.mult, gi_ + [ER], [B_["e4"]] + Ehb[gh]["e4"])
            TT("dve", HR.ap, B_["e1"].ap, B_["e2"].ap, ALU.subtract, [B_["e1"], B_["e2"]] + Ehb[gh]["e2"], [HR] + Ehb[gh]["e1"])
            TT("pool", HI.ap, B_["e3"].ap, B_["e4"].ap, ALU.add, [B_["e3"], B_["e4"]] + Ehb[gh]["e4"], [HI] + Ehb[gh]["e3"])
            hr_rd = [HR] + Ehb[gh]["e1"]; hi_rd = [HI] + Ehb[gh]["e3"]
            CP("act", hpr.ap[:, :, 1:129], HR.ap, hr_rd, [hpr])
            CP("act", hpi.ap[:, :, 1:129], HI.ap, hi_rd, [hpi])
            if blk + 1 < NBLK:
                hl1, hl2 = hls[gh]
                CP("dve", Hl[gh][0].ap, HR.ap[:, :, 127], hr_rd, [Hl[gh][0]])
                CP("dve", Hl[gh][1].ap, HI.ap[:, :, 127], hi_rd, [Hl[gh][1]])
                TT("dve", hl1.ap, Hl[gh][0].ap, c1s.ap[:, gs], ALU.mult, [Hl[gh][0], c1s], [hl1])
                TT("dve", hl2.ap, Hl[gh][1].ap, s1s.ap[:, gs], ALU.mult, [Hl[gh][1], s1s], [hl2])
                TT("dve", Gi[gh][0].ap, hl1.ap, hl2.ap, ALU.subtract, [hl1, hl2], [Gi[gh][0]])
                TT("dve", hl1.ap, Hl[gh][0].ap, s1s.ap[:, gs], ALU.mult, [Hl[gh][0], s1s], [hl1])
                TT("dve", hl2.ap, Hl[gh][1].ap, c1s.ap[:, gs], ALU.mult, [Hl[gh][1], c1s], [hl2])
                TT("dve", Gi[gh][1].ap, hl1.ap, hl2.ap, ALU.add, [hl1, hl2], [Gi[gh][1]])

        def s_stage3(blk, gh):
            cb = blk * 128
            hpr = Hp[0][gh]; hpi = Hp[1][gh]
            Yg = Ygs[gh]
            gs = slice(gh * 8, gh * 8 + 8)
            pY = [psum(), psum()]
            for g8 in range(8):
                g = gh * 8 + g8
                yo = pY[g8 // 4].ap[:, (g8 % 4) * 128:(g8 % 4 + 1) * 128]
                PE(yo, Tg.ap[:, g, :], Ug_all.ap[:, g, cb:cb + 128], True, False, [Tg, Ug_b[g][blk]], [pY[g8 // 4]])
                PE(yo, CmR.ap[:, g, :], hpr.ap[:, g8, 0:128], False, False, [CmR, hpr], [pY[g8 // 4]])
                PE(yo, CmI.ap[:, g, :], hpi.ap[:, g8, 0:128], False, True, [CmI, hpi], [pY[g8 // 4]])
            if blk + 1 < NBLK:
                CP("act", hpr.ap[:, :, 0], Hl[gh][0].ap, [Hl[gh][0]], [hpr])
                CP("act", hpi.ap[:, :, 0], Hl[gh][1].ap, [Hl[gh][1]], [hpi])
            for hf in range(2):
                ACT(Yg.ap[:, hf * 4:hf * 4 + 4, :], pY[hf].ap.rearrange("p (a b) -> p a b", a=4), AF.Gelu, [pY[hf]], [Yg])
            for hf in range(2):
                ps = psum()
                pv = bfv(ps)
                for q4 in range(4):
                    PET(pv[:, q4 * 128:(q4 + 1) * 128], Yg.ap[:, hf * 4 + q4, :], ident_b.ap, [Yg, ident_b], [ps])
                ch0 = gh * 128 + hf * 64
                CP("dve" if hf else "act", Yc.ap[:, :, ch0:ch0 + 64].rearrange("p t (g h) -> p g t h", g=4),
                   pv[:, 0:512].rearrange("p (g t h) -> p g t h", g=4, t=8), [ps], [Ycb[gh][hf]])

        Ycb = [[Buf(), Buf()], [Buf(), Buf()]]
        for blk in range(NBLK):
            for gh in range(2):
                s_stage1(blk, gh)
            for gh in range(2):
                s_stage2(blk, gh)
            for gh in range(2):
                s_stage3(blk, gh)
            for chn in range(2):
                for th_ in range(2):
                    ps = psum()
                    pv = bfv(ps)
                    for q4 in range(4):
                        tau = th_ * 4 + q4
                        PET(pv[:, q4 * 128:(q4 + 1) * 128], Yc.ap[:, tau, chn * 128:(chn + 1) * 128], ident_b.ap, Ycb[chn] + [ident_b], [ps])
                    CP("act" if th_ else "dve", yTb.ap[:, chn, :].rearrange("p (c t) -> p t c", t=8)[:, th_ * 4:th_ * 4 + 4, :],
                       pv[:, 0:512].rearrange("p (t c) -> p t c", t=4), [ps], [yTb])
            DMA("sp", Ys[:, blk * 1024:(blk + 1) * 1024].rearrange("(k p) s -> p k s", p=128), yTb.ap, [yTb], [], "stY")
            if blk == min(1, NBLK - 1):
                mem_kv()
        S_.barrier()

    CW = {}

    def alloc_c_weights():
        A.top = base0
        CW["WG"] = A.tile([128, 8, 2048], BF16, "WG")
        CW["Wgl"] = A.tile([128, 2, 2048], BF16, "Wgl")
        CW["Woa"] = A.tile([128, 4, 1024], BF16, "Woa")
        CW["Wo"] = A.tile([128, 8, 1024], BF16, "Wo")
        CW["Wxq"] = A.tile([128, 8, 1024], BF16, "Wxq")
        CW["Wxo"] = A.tile([128, 8, 1024], BF16, "Wxo")
        lnp = {}
        for nm, (gg, bb) in {"1": (ln1_g, ln1_b), "2": (ln2_g, ln2_b)}.items():
            lnp[nm] = (A.tile([128, 1024], F32, "g" + nm), A.tile([128, 1024], F32, "b" + nm))
            load_bc(lnp[nm][0], gg); load_bc(lnp[nm][1], bb)
        CW["lnp"] = lnp
        wload(CW["WG"].ap, w_in.rearrange("(kc p) n -> p kc n", p=128)[:, :, 800:2848], [CW["WG"]], "wG")
        wload(CW["Wgl"].ap, w_glu.rearrange("(kc p) n -> p kc n", p=128), [CW["Wgl"]], "wgl")
        wload(CW["Woa"].ap, w_oa.rearrange("(kc p) n -> p kc n", p=128), [CW["Woa"]], "woa")
        wload(CW["Wo"].ap, w_o.rearrange("(kc p) n -> p kc n", p=128), [CW["Wo"]], "wo")
        wload(CW["Wxq"].ap, w_xq.rearrange("(kc p) n -> p kc n", p=128), [CW["Wxq"]], "wxq")
        wload(CW["Wxo"].ap, w_xo.rearrange("(kc p) n -> p kc n", p=128), [CW["Wxo"]], "wxo")
        CW["top"] = A.top

    if "B" in phases:
        alloc_c_weights()
        Kh = [A.tile([128, S], BF16, "Kh") for _ in range(2)]
        Vh = [A.tile([128, NB, 128], BF16, "Vh") for _ in range(2)]
        Qh = [A.tile([128, 512], BF16, "Qh") for _ in range(3)]
        rd = [A.tile([128, 512], F32, "rd") for _ in range(2)]
        On = [A.tile([128, 512], BF16, "On") for _ in range(2)]
        scale = 96.0 ** -0.5
        LA = 2
        NSP = 3
        PS_O = [PSB[6], PSB[7]]
        for k_ in range(2):
            MS("pool", Vh[k_].ap[:, :, 64:128], 1.0, [Vh[k_]])
        NPP = 4
        Pp = [A.tile([128, 2, 512], BF16, "Pp") for _ in range(NPP)]
        ptb = [[Buf(), Buf()] for _ in range(NPP)]

        def ld_kv(h):
            DMA("sp", Kh[h % 2].ap[0:96, :], Ks[h], [], [Kh[h % 2]], f"ldK{h % 2}")
            DMA("sp", Vh[h % 2].ap[:, :, 0:65], Vs[h], [], [Vh[h % 2]], f"ldV{h % 2}")
        ld_kv(0)
        qlist = [(h, j) for h in range(8) for j in range(NT)]

        def ld_q(qq):
            if qq < len(qlist):
                hh, jj = qlist[qq]
                DMA("sp", Qh[qq % 3].ap[0:96, :], Qs[hh, :, jj * 512:(jj + 1) * 512], [], [Qh[qq % 3]], f"ldQ{qq % 3}")

        ld_q(0); ld_q(1)
        its = []
        for qq, (h, j) in enumerate(qlist):
            npair = 2 * j + 2
            for pp in range(npair):
                its.append((qq, h, j, pp, npair))
        N = len(its)
        pend = []
        for n in range(N + LA):
            if n < N:
                qq, h, j, pp, npair = its[n]
                kh = Kh[h % 2]
                if pp == 0:
                    ld_q(qq + 2)
                    if j == min(1, NT - 1) and h + 1 < 8:
                        ld_kv(h + 1)
                qh = Qh[qq % 3]
                kb0 = 2 * pp
                i0_ = kb0 - 4 * j
                lo = 128 * i0_ if i0_ > 0 else 0
                sp_ = PSP[n % NSP]
                sb0 = PSB[2 * (n % NSP)]; sb1 = PSB[2 * (n % NSP) + 1]
                pt = Pp[n % NPP]
                PE(sp_[:, lo:512], kh.ap[0:96, kb0 * 128:(kb0 + 1) * 128], qh.ap[0:96, lo:512], True, True, [kh, qh], [sb0])
                PE(sp_[:, 512 + lo:1024], kh.ap[0:96, (kb0 + 1) * 128:(kb0 + 2) * 128], qh.ap[0:96, lo:512], True, True, [kh, qh], [sb1])
                ACT(pt.ap[:, :, lo:512], sp_[:, :].rearrange("p (a b) -> p a b", a=2)[:, :, lo:512], AF.Exp, [sb0, sb1], [pt] + ptb[n % NPP], scale=scale)
                if i0_ >= 0:
                    TT("pool", pt.ap[:, 0, lo:lo + 128], pt.ap[:, 0, lo:lo + 128], tri_b.ap, ALU.mult, [pt, tri_b], [ptb[n % NPP][0]])
                    TT("pool", pt.ap[:, 1, lo + 128:lo + 256], pt.ap[:, 1, lo + 128:lo + 256], tri_b.ap, ALU.mult, [pt, tri_b], [ptb[n % NPP][1]])
            m = n - LA
            if m >= 0:
                qq, h, j, pp, npair = its[m]
                vh = Vh[h % 2]
                kb0 = 2 * pp
                i0_ = kb0 - 4 * j
                lo_a = 128 * i0_ if i0_ > 0 else 0
                lo_b = lo_a + 128 if i0_ >= 0 else 0
                po = PS_O[qq % 2]
                pt = Pp[m % NPP]
                PE(po.ap[:, lo_a:512], vh.ap[:, kb0, :], pt.ap[:, 0, lo_a:512], pp == 0, False, [vh, pt, ptb[m % NPP][0]], [po])
                PE(po.ap[:, lo_b:512], vh.ap[:, kb0 + 1, :], pt.ap[:, 1, lo_b:512], False, pp == npair - 1, [vh, pt, ptb[m % NPP][1]], [po])
                if pp == npair - 1:
                    rdt = rd[qq % 2]; on = On[qq % 2]
                    S_.op("dve", lambda e, r=rdt, p_=po: e.reciprocal(r.ap[64:128, :], p_.ap[64:128, :]), [po.b], [rdt.b])
                    TT("dve", on.ap[0:64, :], po.ap[0:64, :], rdt.ap[64:128, :], ALU.mult, [po, rdt], [on])
                    DMA("pool", Os[h * 64:(h + 1) * 64, j * 512:(j + 1) * 512], on.ap[0:64, :], [on], [], f"stO{qq % 2}")
        S_.barrier()

    if "C" in phases:
        if not CW:
            alloc_c_weights()
        WG = CW["WG"]; Wgl = CW["Wgl"]; Woa = CW["Woa"]; Wo = CW["Wo"]; Wxq = CW["Wxq"]; Wxo = CW["Wxo"]; lnp = CW["lnp"]
        A.top = CW["top"]
        ln_alloc(1)
        xs = MT(A.tile([128, 4, 1024], F32, "xs"))
        hb = MT(A.tile([128, 4, 1024], BF16, "hb"))
        hT = MT(A.tile([128, 8, 512], BF16, "hT"), 8)
        h0Ts = [A.tile([128, 8, 512], BF16, "h0T")] * 2
        mixins = [MT(A.tile([128, 8, 512], BF16, "mixin"), 8) for _ in range(2)]
        qx_ap = hb.ap.rearrange("p n (h s) -> p (n h) s", h=2)
        OTs = [A.tile([128, 4, 512], BF16, "OT") for _ in range(2)]
        yTts = [A.tile([128, 2, 512], BF16, "yTt") for _ in range(2)]
        gsb = [A.tile([128, 512], BF16, "gs") for _ in range(2)]
        gab = [A.tile([128, 512], BF16, "ga")] * 2
        sgb = [A.tile([128, 512], BF16, "sg")] * 2
        u1 = [A.tile([128, 512], F32, "u1")] * 2
        u3 = [A.tile([128, 512], F32, "u3")] * 2
        PTx = [A.tile([128, 512], BF16, "PTx") for _ in range(2)]
        rdx = A.tile([128, 512], F32, "rdx")

        def ld_c_small(t):
            c0 = t * 512
            DMA("sp", h0Ts[t % 2].ap, H0T[:, c0:c0 + 512].rearrange("(k p) s -> p k s", p=128), [], [h0Ts[t % 2]], "ldH0T")
            DMA("sp", OTs[t % 2].ap, Os[:, c0:c0 + 512].rearrange("(k p) s -> p k s", p=128), [], [OTs[t % 2]], f"ldO{t % 2}")
            DMA("sp", yTts[t % 2].ap, Ys[:, c0:c0 + 512].rearrange("(k p) s -> p k s", p=128), [], [yTts[t % 2]], f"ldY{t % 2}")

        def ld_c_x(t):
            DMA("sp", xs.ap, H0[t * 512:(t + 1) * 512, :].rearrange("(n p) d -> p n d", p=128), [], [xs], "xC")

        def chunk_loop(t, crange):
            h0T = h0Ts[t % 2]; OT = OTs[t % 2]; yTt = yTts[t % 2]; mixin = mixins[t % 2]
            for c in crange:
                k = c % 2
                pgs = psum(); pga = psum(); pA = psum(); pB = psum(); pO = psum()
                for kc in range(8):
                    PE(pgs.ap, WG.ap[:, kc, c * 128:(c + 1) * 128], h0T.ap[:, kc, :], kc == 0, kc == 7, [WG, h0T], [pgs])
                for kc in range(8):
                    PE(pga.ap, WG.ap[:, kc, 1024 + c * 128:1024 + (c + 1) * 128], h0T.ap[:, kc, :], kc == 0, kc == 7, [WG, h0T], [pga])
                for kc in range(2):
                    PE(pA.ap, Wgl.ap[:, kc, c * 128:(c + 1) * 128], yTt.ap[:, kc, :], kc == 0, kc == 1, [Wgl, yTt], [pA])
                for kc in range(2):
                    PE(pB.ap, Wgl.ap[:, kc, 1024 + c * 128:1024 + (c + 1) * 128], yTt.ap[:, kc, :], kc == 0, kc == 1, [Wgl, yTt], [pB])
                for kc in range(4):
                    PE(pO.ap, Woa.ap[:, kc, c * 128:(c + 1) * 128], OT.ap[:, kc, :], kc == 0, kc == 3, [Woa, OT], [pO])
                ACT(gsb[k].ap, pgs.ap, AF.Sigmoid, [pgs], [gsb[k]])
                ACT(gab[k].ap, pga.ap, AF.Sigmoid, [pga], [gab[k]])
                ACT(sgb[k].ap, pB.ap, AF.Sigmoid, [pB], [sgb[k]])
                TT("dve", u1[k].ap, pA.ap, sgb[k].ap, ALU.mult, [pA, sgb[k]], [u1[k]])
                TT("dve", u1[k].ap, u1[k].ap, gsb[k].ap, ALU.mult, [u1[k], gsb[k]], [u1[k]])
                TT("dve", u3[k].ap, pO.ap, gab[k].ap, ALU.mult, [pO, gab[k]], [u3[k]])
                TT("pool", mixin.ap[:, c, :], u1[k].ap, u3[k].ap, ALU.add, [u1[k], u3[k]], [mixin.bufs[c]])

        def wo_ln1(t):
            mixin = mixins[t % 2]
            for n in range(4):
                for hf in range(2):
                    ps = psum()
                    for kc in range(8):
                        PE(ps.ap, mixin.ap[:, kc, n * 128:(n + 1) * 128], Wo.ap[:, kc, hf * 512:(hf + 1) * 512], kc == 0, kc == 7, [mixin, Wo], [ps])
                    STT("dve", xs.ap[:, n, hf * 512:(hf + 1) * 512], xs.ap[:, n, hf * 512:(hf + 1) * 512], DN_ALPHA, ps.ap, ALU.mult, ALU.add, [xs.bufs[n], ps], [xs.bufs[n]])

        def xattn(t):
            c0 = t * 512
            mixin = mixins[t % 2]
            transpose_blocks(hb, hT)
            for c in range(8):
                ps = psum()
                for kc in range(8):
                    PE(ps.ap, Wxq.ap[:, kc, c * 128:(c + 1) * 128], hT.ap[:, kc, :], kc == 0, kc == 7, [Wxq, hT.bufs[kc]], [ps])
                CP("act" if c % 2 else "dve", qx_ap[:, c, :], ps.ap, [ps], [hb.bufs[c // 2]])
            for hx in range(4):
                qb = [hb.bufs[hx]]
                for mb in range(2):
                    ps = psum()
                    for dc in range(2):
                        PE(ps.ap, KmT.ap[:, 2 * hx + dc, mb * 128:(mb + 1) * 128], qx_ap[:, 2 * hx + dc, :], dc == 0, dc == 1, [KmT] + qb, [ps])
                    ACT(PTx[mb].ap, ps.ap, AF.Exp, [ps], [PTx[mb]], scale=1.0 / 16.0)
                pd = psum()
                for mb in range(2):
                    PE(pd.ap, ones_b.ap, PTx[mb].ap, mb == 0, mb == 1, [ones_b, PTx[mb]], [pd])
                ACT(rdx.ap, pd.ap, AF.Ln, [pd], [rdx])
                ACT(rdx.ap, rdx.ap, AF.Exp, [rdx], [rdx], scale=-1.0)
                for dc in range(2):
                    ps = psum()
                    for mb in range(2):
                        PE(ps.ap, Vm.ap[:, mb, (2 * hx + dc) * 128:(2 * hx + dc + 1) * 128], PTx[mb].ap, mb == 0, mb == 1, [Vm, PTx[mb]], [ps])
                    TT("dve", mixin.ap[:, 2 * hx + dc, :], ps.ap, rdx.ap, ALU.mult, [ps, rdx], [mixin.bufs[2 * hx + dc]])
            for n in range(4):
                for hf in range(2):
                    ps = psum()
                    for kc in range(8):
                        PE(ps.ap, mixin.ap[:, kc, n * 128:(n + 1) * 128], Wxo.ap[:, kc, hf * 512:(hf + 1) * 512], kc == 0, kc == 7, [mixin, Wxo], [ps])
                    STT("dve", xs.ap[:, n, hf * 512:(hf + 1) * 512], xs.ap[:, n, hf * 512:(hf + 1) * 512], DN_ALPHA, ps.ap, ALU.mult, ALU.add, [xs.bufs[n], ps], [xs.bufs[n]])
            ln4(xs, lnp["2"][0], lnp["2"][1], None)
            DMA("sp", H2[c0:c0 + 512, :].rearrange("(n p) d -> p n d", p=128), xs.ap, [xs], [], "stH2")

        ld_c_small(0)
        ld_c_x(0)
        chunk_loop(0, range(8))
        for t in range(NT):
            wo_ln1(t)
            if t + 1 < NT:
                ld_c_small(t + 1)
                chunk_loop(t + 1, range(0, 2))
            cps = ln4(xs, lnp["1"][0], lnp["1"][1], hb, defer=True)
            if t + 1 < NT:
                chunk_loop(t + 1, range(2, 5))
            cps()
            xattn(t)
            if t + 1 < NT:
                ld_c_x(t + 1)
                chunk_loop(t + 1, range(5, 8))
        S_.barrier()

    if "D" in phases:
        A.top = base00
        Wup = A.tile([128, 8, 4096], BF16, "Wup")
        Wdn = A.tile([128, 32, 1024], BF16, "Wdn")
        g3 = A.tile([128, 1024], F32, "g3"); b3 = A.tile([128, 1024], F32, "b3")
        load_bc(g3, ln3_g); load_bc(b3, ln3_b)
        for q in range(4):
            wload(Wup.ap[:, :, q * 1024:(q + 1) * 1024], w_up.rearrange("(kc p) n -> p kc n", p=128)[:, :, q * 1024:(q + 1) * 1024], [Wup], "wup")
            wload(Wdn.ap[:, q * 8:(q + 1) * 8, :], w_down.rearrange("(kc p) n -> p kc n", p=128)[:, q * 8:(q + 1) * 8, :], [Wdn], "wdn")
        ln_alloc(1)
        xs = MT(A.tile([128, 4, 1024], F32, "xs"))
        hb = MT(A.tile([128, 4, 1024], BF16, "hb"))
        hT = A.tile([128, 8, 512], BF16, "hT")
        xs2 = MT(A.tile([128, 4, 1024], F32, "xs2"))
        xss = [xs, xs2]
        hids = [MT(A.tile([128, 8, 512], BF16, "hid"), 8) for _ in range(2)]
        tmpr = [A.tile([128, 512], F32, "tmpr") for _ in range(2)]
        ob = Buf("out")

        def ld_h2(t):
            DMA("sp", xss[t % 2].ap, H2[t * 512:(t + 1) * 512, :].rearrange("(n p) d -> p n d", p=128), [], [xss[t % 2]], f"xD{t % 2}")

        ld_h2(0)
        pctr = 0

        def conv_tr(t):
            xq_ = xss[t % 2]
            for n in range(4):
                CP("act" if n % 2 else "dve", hb.ap[:, n, :], xq_.ap[:, n, :], [xq_.bufs[n]], [hb.bufs[n]])
            transpose_blocks(hb, hT)

        def up_pass(ps_, hid):
            for hl in range(8):
                hc = ps_ * 8 + hl
                ps = psum()
                for kc in range(8):
                    PE(ps.ap, Wup.ap[:, kc, hc * 128:(hc + 1) * 128], hT.ap[:, kc, :], kc == 0, kc == 7, [Wup, hT], [ps])
                if hl % 2 == 0:
                    ACT(tmpr[0].ap, ps.ap, AF.Relu, [ps], [tmpr[0]])
                    TT("dve", hid.ap[:, hl, :], tmpr[0].ap, tmpr[0].ap, ALU.mult, [tmpr[0]], [hid.bufs[hl]])
                else:
                    TS("dve", tmpr[1].ap, ps.ap, 0.0, None, ALU.max, None, [ps], [tmpr[1]])
                    ACT(hid.ap[:, hl, :], tmpr[1].ap, AF.Square, [tmpr[1]], [hid.bufs[hl]])

        def down_pass(ps_, hid, xs):
            for n in range(4):
                for hf in range(2):
                    ps = psum()
                    for hl in range(8):
                        PE(ps.ap, hid.ap[:, hl, n * 128:(n + 1) * 128], Wdn.ap[:, ps_ * 8 + hl, hf * 512:(hf + 1) * 512], hl == 0, hl == 7, [hid.bufs[hl], Wdn], [ps])
                    xv = xs.ap[:, n, hf * 512:(hf + 1) * 512]
                    STT("dve", xv, xv, DN_ALPHA if ps_ == 0 else 1.0, ps.ap, ALU.mult, ALU.add, [xs.bufs[n], ps], [xs.bufs[n]])

        conv_tr(0)
        for t in range(NT):
            c0 = t * 512
            xs = xss[t % 2]
            if t + 1 < NT:
                ld_h2(t + 1)
            up_pass(0, hids[0])
            for ps_ in range(4):
                if ps_ + 1 < 4:
                    up_pass(ps_ + 1, hids[(ps_ + 1) % 2])
                down_pass(ps_, hids[ps_ % 2], xs)
            if t + 1 < NT:
                conv_tr(t + 1)
            ln4(xs, g3, b3, None)
            DMA("sp", out[c0:c0 + 512, :].rearrange("(n p) d -> p n d", p=128), xs.ap, [xs], [ob], "stOut")
        final_bufs.append(ob)

    keys = S_.finalize(final_waits=final_bufs)
    sems = {k: es.enter_context(nc.semaphore(f"s{i}")) for i, k in enumerate(keys)}
    with nc.Block() as block:
        S_.emit(block, sems)
    es.close()
    return nc


def _consts():
    ident = np.eye(128, dtype=np.float32)
    idx = np.arange(128)
    cm = (idx[None, :] // 16 >= idx[:, None] // 16).astype(np.float32)
    tri = (idx[None, :] >= idx[:, None]).astype(np.float32)
    rope = np.zeros((128, 3), np.float32)
    rope[:, 2] = np.float32(math.pi / 2)
    inv = (10000.0 ** (-np.arange(0, 32, 2, dtype=np.float32) / np.float32(32))).astype(np.float32)
    for r in range(32):
        rope[64 + r, 0] = inv[r % 16]
        rope[64 + r, 1] = -1.0 if r < 16 else 1.0
    kvec = np.tile(np.arange(-7, 9, dtype=np.float32)[None, :], (64, 1))
    cvec = np.tile(np.arange(128, dtype=np.float32)[None, :], (64, 1))
    return {"c_ident": ident, "c_cm": cm, "c_tri": tri, "c_rope": rope, "c_kvec": kvec, "c_cvec": cvec}


def _core_inputs(inp, b, S):
    f = lambda a: np.ascontiguousarray(a, dtype=np.float32)
    m = {
        "x": f(inp["x"][b, :S]), "mem": f(inp["mem"][b]),
        "positions": np.ascontiguousarray(inp["positions"][b, :S].reshape(1, S).astype(np.int32)),
        "ln_in_g": f(inp["ln_in_g"].reshape(1, 1024)), "ln_in_b": f(inp["ln_in_b"].reshape(1, 1024)),
        "w_in": f(inp["w_in"][0]),
        "s5_lam_re": f(inp["s5_lam_re"][0]), "s5_lam_im": f(inp["s5_lam_im"][0]), "s5_log_dt": f(inp["s5_log_dt"].reshape(1, 16)),
        "s5_b_re": f(inp["s5_b_re"][0]), "s5_b_im": f(inp["s5_b_im"][0]),
        "s5_c_re": f(inp["s5_c_re"][0]), "s5_c_im": f(inp["s5_c_im"][0]),
        "s5_d": f(inp["s5_d"].reshape(16, 16)),
        "w_glu": f(inp["w_glu"][0]), "q_norm_g": f(inp["q_norm_g"].reshape(256, 1)), "w_uq": f(inp["w_uq"][0]),
        "kv_norm_g": f(inp["kv_norm_g"].reshape(256, 1)), "w_ukv": f(inp["w_ukv"][0]),
        "w_oa": f(inp["w_oa"][0]), "w_o": f(inp["w_o"][0]),
        "ln1_g": f(inp["ln1_g"].reshape(1, 1024)), "ln1_b": f(inp["ln1_b"].reshape(1, 1024)),
        "w_xq": f(inp["w_xq"][0]), "w_xk": f(inp["w_xk"][0]), "w_xv": f(inp["w_xv"][0]), "w_xo": f(inp["w_xo"][0]),
        "ln2_g": f(inp["ln2_g"].reshape(1, 1024)), "ln2_b": f(inp["ln2_b"].reshape(1, 1024)),
        "w_up": f(inp["w_up"][0]), "w_down": f(inp["w_down"][0]),
        "ln3_g": f(inp["ln3_g"].reshape(1, 1024)), "ln3_b": f(inp["ln3_b"].reshape(1, 1024)),
    }
    m.update(_consts())
    return m


def kernel(**inputs):
    S = inputs["x"].shape[1]
    B = inputs["x"].shape[0]
    nc = build(S)
    in_maps = [_core_inputs(inputs, b, S) for b in range(B)]
    res = run_bass_kernel_spmd(nc, in_maps, core_ids=list(range(B)))
    return np.stack([np.asarray(r["out"], dtype=np.float32) for r in res.results], axis=0)
```

```python
import contextlib
import math
import numpy as np
import ml_dtypes
import concourse.bass as bass
import concourse.mybir as mybir
from concourse.bass_utils import run_bass_kernel_spmd

F32 = mybir.dt.float32
BF16 = mybir.dt.bfloat16
I32 = mybir.dt.int32
AF = mybir.ActivationFunctionType
ALU = mybir.AluOpType

ENGS = ("pe", "act", "dve", "pool", "sp")
EPOCH = 12000
PI = math.pi
TWO_PI = 2.0 * math.pi
CW1, CW2, CW3 = 6.28125, 0.0019350051879882812, 3.019916050561733e-07
DN_ALPHA = 2.0 ** 0.25
LN_EPS = 1e-5
RMS_EPS = 1e-6


class Buf:
    __slots__ = ("name", "w", "r")

    def __init__(self, name=""):
        self.name = name
        self.w = None
        self.r = []


class Op:
    __slots__ = ("eng", "emit", "deps", "stream", "pos", "isdma", "vc", "waits", "signal", "cnt", "ep")


class Sched:
    def __init__(self, nc):
        self.nc = nc
        self.ops = []
        self.stream_ops = {e: [] for e in ENGS}

    def _record(self, eng, emit, reads, writes, dma_stream=None, extra_deps=()):
        op = Op()
        op.eng = eng
        op.emit = emit
        op.isdma = dma_stream is not None
        op.stream = dma_stream if op.isdma else eng
        op.signal = op.isdma
        deps = set(extra_deps)
        for b in reads:
            if b.w is not None:
                deps.add(b.w)
        for b in writes:
            if b.w is not None:
                pw = self.ops[b.w]
                if op.isdma:
                    skip = pw.stream == op.stream
                else:
                    skip = eng == "pe" and pw.stream == "pe"
                if not skip:
                    deps.add(b.w)
            for r in b.r:
                deps.add(r)
        idx = len(self.ops)
        for b in writes:
            b.w = idx
            b.r = []
        for b in reads:
            if b.w != idx:
                b.r.append(idx)
        op.deps = deps
        if emit is None:
            op.pos = 0
        else:
            so = self.stream_ops.setdefault(op.stream, [])
            so.append(idx)
            op.pos = len(so)
        self.ops.append(op)
        return idx

    def op(self, eng, emit, reads=(), writes=()):
        return self._record(eng, emit, reads, writes)

    def dma(self, queue, emit, reads=(), writes=(), stream=None):
        return self._record(queue, emit, reads, writes, dma_stream="dma:" + stream)

    def barrier(self):
        last = [so[-1] for so in self.stream_ops.values() if so]
        for e in ENGS:
            self._record(e, None, (), (), extra_deps=last)

    def finalize(self, final_waits=()):
        ops = self.ops
        known = {e: {} for e in ENGS}
        for op in ops:
            kn = known[op.eng]
            need = {}
            for d in op.deps:
                p = ops[d]
                if kn.get(p.stream, 0) < p.pos and need.get(p.stream, 0) < p.pos:
                    need[p.stream] = p.pos
            waits = []
            for st, pos in need.items():
                if kn.get(st, 0) >= pos:
                    continue
                p = ops[self.stream_ops[st][pos - 1]]
                p.signal = True
                waits.append(p)
                for k, v in p.vc.items():
                    if kn.get(k, 0) < v:
                        kn[k] = v
                kn[st] = pos
            op.waits = waits
            op.vc = dict(kn)
        self.final = []
        for b in final_waits:
            if b.w is not None:
                self.final.append(ops[b.w])
        cnt = {}
        keys = []
        for op in ops:
            if op.emit is None:
                op.signal = False
            if op.signal:
                ep, c = cnt.get(op.stream, (0, 0))
                c += 16 if op.isdma else 1
                if c > EPOCH:
                    ep += 1
                    c = 16 if op.isdma else 1
                cnt[op.stream] = (ep, c)
                op.ep = ep
                op.cnt = c
                if (op.stream, ep) not in keys:
                    keys.append((op.stream, ep))
        return keys

    def emit(self, block, sems):
        ops = self.ops
        per_eng = {e: [] for e in ENGS}
        for op in ops:
            per_eng[op.eng].append(op)

        def run(engh, lst, is_sp):
            for op in lst:
                for p in op.waits:
                    engh.wait_ge(sems[(p.stream, p.ep)], p.cnt)
                if op.emit is None:
                    continue
                ins = op.emit(engh)
                if op.signal:
                    ins.then_inc(sems[(op.stream, op.ep)], 16 if op.isdma else 1)
            if is_sp:
                for p in self.final:
                    engh.wait_ge(sems[(p.stream, p.ep)], p.cnt)

        @block.tensor
        def _(e):
            run(e, per_eng["pe"], False)

        @block.scalar
        def _(e):
            run(e, per_eng["act"], False)

        @block.vector
        def _(e):
            run(e, per_eng["dve"], False)

        @block.gpsimd
        def _(e):
            run(e, per_eng["pool"], False)

        @block.sync
        def _(e):
            run(e, per_eng["sp"], True)


class T:
    __slots__ = ("ap", "b")

    def __init__(self, ap, b=None):
        self.ap = ap
        self.b = b if b is not None else Buf()

    def __getitem__(self, k):
        return self.ap[k]


class MT:
    __slots__ = ("ap", "bufs")

    def __init__(self, t, n=4):
        self.ap = t.ap
        self.bufs = [Buf() for _ in range(n)]

    def blk(self, n):
        return T(self.ap[:, n, :], self.bufs[n])


_DSZ = {F32: 4, BF16: 2, I32: 4}


class Arena:
    def __init__(self, ar, cap):
        self.ar = ar
        self.cap = cap
        self.top = 0
        self.peak = 0

    def tile(self, shape, dt, name=""):
        n = 1
        for s in shape[1:]:
            n *= s
        nel = (n * _DSZ[dt] + 1) // 2
        nel = (nel + 15) // 16 * 16
        off = self.top
        self.top += nel
        self.peak = max(self.peak, self.top)
        assert self.top <= self.cap, f"arena overflow {name} {self.top}"
        v = self.ar[0:shape[0], off:off + (n * _DSZ[dt]) // 2]
        if dt != BF16:
            v = v.bitcast(dt)
        if len(shape) == 3:
            v = v.rearrange("p (a b) -> p a b", a=shape[1])
        elif len(shape) == 4:
            v = v.rearrange("p (a b c) -> p a b c", a=shape[1], b=shape[2])
        return T(v, Buf(name))


def build(S, phases="ASBCD", dbg=False):
    NT = S // 512
    NB = S // 128
    NCH = S // 8
    NBLK = S // 1024
    nc = bass.Bass("TRN2", target_bir_lowering=False)

    def din(name, shape, dt=F32):
        return nc.dram_tensor(name, shape, dt, kind="ExternalInput").ap()

    x = din("x", [S, 1024])
    mem = din("mem", [256, 1024])
    pos = din("positions", [1, S], I32)
    ln_in_g = din("ln_in_g", [1, 1024]); ln_in_b = din("ln_in_b", [1, 1024])
    w_in = din("w_in", [1024, 2848])
    lam_re = din("s5_lam_re", [16, 64]); lam_im = din("s5_lam_im", [16, 64]); log_dt = din("s5_log_dt", [1, 16])
    b_re = din("s5_b_re", [16, 64, 16]); b_im = din("s5_b_im", [16, 64, 16])
    c_re = din("s5_c_re", [16, 16, 64]); c_im = din("s5_c_im", [16, 16, 64])
    s5_d = din("s5_d", [16, 16])
    w_glu = din("w_glu", [256, 2048])
    q_norm_g = din("q_norm_g", [256, 1]); w_uq = din("w_uq", [256, 768])
    kv_norm_g = din("kv_norm_g", [256, 1]); w_ukv = din("w_ukv", [256, 1024])
    w_oa = din("w_oa", [512, 1024]); w_o = din("w_o", [1024, 1024])
    ln1_g = din("ln1_g", [1, 1024]); ln1_b = din("ln1_b", [1, 1024])
    w_xq = din("w_xq", [1024, 1024]); w_xk = din("w_xk", [1024, 1024])
    w_xv = din("w_xv", [1024, 1024]); w_xo = din("w_xo", [1024, 1024])
    ln2_g = din("ln2_g", [1, 1024]); ln2_b = din("ln2_b", [1, 1024])
    w_up = din("w_up", [1024, 4096]); w_down = din("w_down", [4096, 1024])
    ln3_g = din("ln3_g", [1, 1024]); ln3_b = din("ln3_b", [1, 1024])
    c_ident = din("c_ident", [128, 128]); c_cm = din("c_cm", [128, 128]); c_tri = din("c_tri", [128, 128])
    c_rope = din("c_rope", [128, 3]); c_kvec = din("c_kvec", [64, 16]); c_cvec = din("c_cvec", [64, 128])
    out = nc.dram_tensor("out", [S, 1024], F32, kind="ExternalOutput").ap()
    skind = "ExternalOutput" if dbg else "Internal"
    Qs = nc.dram_tensor("Qs", [8, 96, S], BF16, kind=skind).ap()
    Ks = nc.dram_tensor("Ks", [8, 96, S], BF16, kind=skind).ap()
    Vs = nc.dram_tensor("Vs", [8, 128, NB, 65], BF16, kind=skind).ap()
    Os = nc.dram_tensor("Os", [512, S], BF16, kind=skind).ap()
    Ys = nc.dram_tensor("Ys", [256, S], BF16, kind=skind).ap()
    H2 = nc.dram_tensor("H2", [S, 1024], F32, kind=skind).ap()
    H0 = nc.dram_tensor("H0", [S, 1024], F32).ap()
    H0T = nc.dram_tensor("H0T", [1024, S], BF16).ap()

    es = contextlib.ExitStack()
    CAP = 106400
    ar = es.enter_context(nc.sbuf_tensor("arena", [128, CAP], BF16))
    A = Arena(ar, CAP)
    PSB = []
    PSP = []
    for i in range(4):
        pt = es.enter_context(nc.psum_tensor(f"ps{i}", [128, 1024], F32))
        PSP.append(pt)
        PSB.append(T(pt[:, 0:512], Buf(f"ps{2 * i}")))
        PSB.append(T(pt[:, 512:1024], Buf(f"ps{2 * i + 1}")))
    S_ = Sched(nc)
    state = {"ps": 0, "dq": 0}

    def psum():
        i = state["ps"]
        state["ps"] = (i + 1) % 8
        return PSB[i]

    def bfv(ps):
        return ps.ap.bitcast(BF16)

    def bl(ts):
        o = []
        for t in ts:
            if isinstance(t, MT):
                o.extend(t.bufs)
            elif isinstance(t, T):
                o.append(t.b)
            else:
                o.append(t)
        return o

    def PE(o, lhsT, rhs, st, sp, R, W):
        S_.op("pe", lambda e: e.matmul(o, lhsT=lhsT, rhs=rhs, start=st, stop=sp), bl(R), bl(W))

    def PET(o, i, ident, R, W):
        S_.op("pe", lambda e: e.transpose(o, i, ident), bl(R), bl(W))

    def ACT(o, i, func, R, W, scale=1.0, bias=0.0, accum=None):
        if accum is None:
            S_.op("act", lambda e: e.activation(out=o, in_=i, func=func, bias=bias, scale=scale), bl(R), bl(W))
        else:
            S_.op("act", lambda e: e.activation(out=o, in_=i, func=func, bias=bias, scale=scale, accum_out=accum), bl(R), bl(W))

    def TT(eng, o, a, b, op, R, W):
        S_.op(eng, lambda e: e.tensor_tensor(out=o, in0=a, in1=b, op=op), bl(R), bl(W))

    def TS(eng, o, a, s1, s2, op0, op1, R, W):
        if op1 is None:
            S_.op(eng, lambda e: e.tensor_scalar(out=o, in0=a, scalar1=s1, scalar2=None, op0=op0), bl(R), bl(W))
        else:
            S_.op(eng, lambda e: e.tensor_scalar(out=o, in0=a, scalar1=s1, scalar2=s2, op0=op0, op1=op1), bl(R), bl(W))

    def STT(eng, o, a, s, b, op0, op1, R, W):
        eng = "dve"
        S_.op(eng, lambda e: e.scalar_tensor_tensor(out=o, in0=a, scalar=s, in1=b, op0=op0, op1=op1), bl(R), bl(W))

    def CP(eng, o, i, R, W):
        if eng == "act":
            S_.op("act", lambda e: e.activation(out=o, in_=i, func=AF.Copy), bl(R), bl(W))
        else:
            S_.op(eng, lambda e: e.tensor_copy(out=o, in_=i), bl(R), bl(W))

    def MS(eng, o, v, W):
        S_.op(eng, lambda e: e.memset(o, v), (), bl(W))

    def DMA(q, o, i, R, W, stream, nonc=False):
        if nonc:
            S_.dma(q, lambda e: e.dma_start(out=o, in_=i, allow_slow_non_contiguous=True), bl(R), bl(W), stream=stream)
        else:
            S_.dma(q, lambda e: e.dma_start(out=o, in_=i), bl(R), bl(W), stream=stream)

    def rr(eng, xt, kf, ki, R0):
        TS(eng, kf.ap, xt.ap, 1.0 / TWO_PI, 0.5, ALU.mult, ALU.add, [xt] + R0, [kf])
        CP(eng, ki.ap, kf.ap, [kf], [ki])
        CP(eng, kf.ap, ki.ap, [ki], [kf])
        for cc in (CW1, CW2, CW3):
            STT(eng, xt.ap, kf.ap, -cc, xt.ap, ALU.mult, ALU.add, [kf, xt], [xt])
        TS(eng, kf.ap, xt.ap, -PI, TWO_PI, ALU.is_lt, ALU.mult, [xt], [kf])
        TT(eng, xt.ap, xt.ap, kf.ap, ALU.add, [xt, kf], [xt])
        TS(eng, xt.ap, xt.ap, -PI, PI, ALU.max, ALU.min, [xt], [xt])

    ident_f = A.tile([128, 128], F32, "ident_f")
    ident_b = A.tile([128, 128], BF16, "ident_b")
    ones_f = A.tile([128, 128], F32, "ones_f")
    ones_b = A.tile([128, 128], BF16, "ones_b")
    tri_b = A.tile([128, 128], BF16, "tri_b")
    cm_f = A.tile([128, 128], F32, "cm_f")
    ropec = A.tile([128, 3], F32, "ropec")
    DMA("sp", ident_f.ap, c_ident, [], [ident_f], "c0")
    DMA("sp", cm_f.ap, c_cm, [], [cm_f], "c1")
    DMA("sp", ropec.ap, c_rope, [], [ropec], "c2")
    DMA("pool", ident_b.ap, c_ident, [], [ident_b], "c3")
    DMA("pool", tri_b.ap, c_tri, [], [tri_b], "c4")
    mhalf = A.tile([128, 4], F32, "mhalf")
    MS("dve", mhalf.ap, -0.5, [mhalf])
    MS("dve", ones_f.ap, 1.0, [ones_f])
    MS("dve", ones_b.ap, 1.0, [ones_b])
    base00 = A.top
    KmT = A.tile([128, 8, 256], BF16, "KmT")
    Vm = A.tile([128, 2, 1024], BF16, "Vm")
    base0 = A.top
    Ug_all = A.tile([128, 16, NCH], BF16, "Ug_all")
    Ug_b = [[Buf() for _ in range(NBLK)] for _ in range(16)]
    base_top = A.top

    def load_bc(dst, src):
        DMA("sp", dst.ap, src.partition_broadcast(128)[:, 0, :], [], [dst], "bc_" + dst.b.name)

    ln_tmp = {}

    def ln_alloc(ntmp):
        ln_tmp["st"] = [A.tile([128, 12], F32, "st") for _ in range(4)]
        ln_tmp["stb"] = [[Buf(), Buf()] for _ in range(4)]
        ln_tmp["mv"] = [A.tile([128, 2], F32, "mv") for _ in range(4)]
        ln_tmp["rs"] = A.tile([128, 4, 2], F32, "rs")

    def ln4(xs, g_bc, b_bc, hb, defer=False):
        st = ln_tmp["st"]; stb = ln_tmp["stb"]; mv = ln_tmp["mv"]; rs = ln_tmp["rs"]
        for n in range(4):
            xb = xs.blk(n)
            S_.op("dve", lambda e, n=n, xb=xb: e.bn_stats(st[n].ap[:, 0:6], xb.ap[:, 0:512]), [xb.b], [stb[n][0]])
            S_.op("dve", lambda e, n=n, xb=xb: e.bn_stats(st[n].ap[:, 6:12], xb.ap[:, 512:1024]), [xb.b], [stb[n][1]])
        for n in range(4):
            S_.op("dve", lambda e, n=n: e.bn_aggr(mv[n].ap, st[n].ap), stb[n], [mv[n].b])
        for n in range(4):
            TS("dve", rs.ap[:, n, 0:1], mv[n].ap[:, 1:2], LN_EPS, None, ALU.add, None, [mv[n]], [rs])
        TT("pool", rs.ap[:, :, 0], rs.ap[:, :, 0], mhalf.ap[:, 0:4], ALU.pow, [rs, mhalf], [rs])
        for n in range(4):
            STT("dve", rs.ap[:, n, 1:2], mv[n].ap[:, 0:1], -1.0, rs.ap[:, n, 0:1], ALU.mult, ALU.mult, [mv[n], rs], [rs])
        for n in range(4):
            xb = xs.blk(n)
            TS("dve", xb.ap, xb.ap, rs.ap[:, n, 0:1], rs.ap[:, n, 1:2], ALU.mult, ALU.add, [xb, rs], [xb])
        for n in range(4):
            xb = xs.blk(n)
            TT("dve", xb.ap, xb.ap, g_bc.ap, ALU.mult, [xb, g_bc], [xb])
        for n in range(4):
            xb = xs.blk(n)
            TT("pool" if n % 2 == 0 else "dve", xb.ap, xb.ap, b_bc.ap, ALU.add, [xb, b_bc], [xb])

        def copies():
            if hb is not None:
                for n in range(4):
                    xb = xs.blk(n)
                    hbb = hb.blk(n)
                    CP("act", hbb.ap, xb.ap, [xb], [hbb])
        if defer:
            return copies
        copies()

    def transpose_blocks(hb, hT):
        for c in range(8):
            ps = psum()
            pv = bfv(ps)
            for n in range(4):
                PET(pv[:, n * 128:(n + 1) * 128], hb.ap[:, n, c * 128:(c + 1) * 128], ident_b.ap, [hb, ident_b], [ps])
            CP("act" if c % 2 else "dve", hT.ap[:, c, :], pv[:, 0:512], [ps], [hT.bufs[c]] if isinstance(hT, MT) else [hT])

    def wload(dst_ap, src_ap, W, stream, nonc=False):
        DMA("pool", dst_ap, src_ap, [], W, stream, nonc)

    final_bufs = []

    if "A" in phases:
        A.top = base_top
        WA = A.tile([128, 8, 960], BF16, "WA")
        Wuq = A.tile([128, 2, 768], BF16, "Wuq")
        Wuqs = A.tile([128, 2, 768], BF16, "Wuqs")
        Wk = A.tile([128, 2, 512], BF16, "Wk")
        Wv = A.tile([128, 2, 512], BF16, "Wv")
        qg = A.tile([128, 2], F32, "qg"); kvg = A.tile([128, 2], F32, "kvg")
        g_in = A.tile([128, 1024], F32, "g_in"); b_in = A.tile([128, 1024], F32, "b_in")
        w_in_r = w_in.rearrange("(kc p) n -> p kc n", p=128)
        MS("pool", WA.ap[:, :, 768:960], 0.0, [WA])
        MS("pool", Wuqs.ap, 0.0, [Wuqs])
        wload(WA.ap[:, :, 0:768], w_in_r[:, :, 0:768], [WA], "wA")
        wload(WA.ap[:, :, 832:864], w_in_r[:, :, 768:800], [WA], "wA")
        wload(WA.ap[:, :, 928:944], w_in_r[:, :, 784:800], [WA], "wA")
        wload(WA.ap[:, :, 944:960], w_in_r[:, :, 768:784], [WA], "wA")
        w_uq_r = w_uq.rearrange("(kc p) n -> p kc n", p=128)
        wload(Wuq.ap, w_uq_r, [Wuq], "wuq")
        for h in range(8):
            wload(Wuqs.ap[:, :, h * 96 + 64:h * 96 + 80], w_uq_r[:, :, h * 96 + 80:h * 96 + 96], [Wuqs], "wuqs")
            wload(Wuqs.ap[:, :, h * 96 + 80:h * 96 + 96], w_uq_r[:, :, h * 96 + 64:h * 96 + 80], [Wuqs], "wuqs")
        w_ukv_r = w_ukv.rearrange("(kc p) n -> p kc n", p=128)
        for h in range(8):
            wload(Wk.ap[:, :, h * 64:(h + 1) * 64], w_ukv_r[:, :, h * 128:h * 128 + 64], [Wk], "wk")
            wload(Wv.ap[:, :, h * 64:(h + 1) * 64], w_ukv_r[:, :, h * 128 + 64:h * 128 + 128], [Wv], "wv")
        DMA("sp", qg.ap, q_norm_g.rearrange("(c p) o -> p (c o)", p=128), [], [qg], "qg", nonc=True)
        DMA("sp", kvg.ap, kv_norm_g.rearrange("(c p) o -> p (c o)", p=128), [], [kvg], "kvg", nonc=True)
        load_bc(g_in, ln_in_g); load_bc(b_in, ln_in_b)
        ln_alloc(2)
        xt = [MT(A.tile([128, 4, 1024], F32, "xt")) for _ in range(2)]
        hb = MT(A.tile([128, 4, 1024], BF16, "hb"))
        hT = MT(A.tile([128, 8, 512], BF16, "hT"), 8)
        cT = [A.tile([128, 512], F32, "cT") for _ in range(4)]
        sq = [A.tile([128, 512], F32, "sq") for _ in range(4)]
        epsc = A.tile([128, 1], F32, "epsc")
        MS("dve", epsc.ap, RMS_EPS, [epsc])

        rstd = [A.tile([128, 512], F32, "rstd") for _ in range(2)]
        cn = [A.tile([128, 2, 512], BF16, "cn") for _ in range(2)]
        Qt = MT(A.tile([128, 8, 512], BF16, "Qt"), 16)
        Kt = MT(A.tile([128, 8, 512], BF16, "Kt"), 9)
        Vt = MT(A.tile([128, 4, 8, 65], BF16, "Vt"), 4)
        Uc = MT(A.tile([128, 16, 8, 16], BF16, "Uc"), 4)
        posi = A.tile([128, 512], I32, "posi")
        angS = A.tile([128, 512], F32, "angS"); angC = A.tile([128, 512], F32, "angC")
        rkf = A.tile([128, 512], F32, "rkf"); rki = A.tile([128, 512], I32, "rki")
        rkf2 = A.tile([128, 512], F32, "rkf2"); rki2 = A.tile([128, 512], I32, "rki2")
        ropeCs = [A.tile([128, 512], F32, "ropeC") for _ in range(2)]
        ropeSs = [A.tile([128, 512], F32, "ropeS") for _ in range(2)]
        t1 = [A.tile([128, 512], F32, "t1") for _ in range(2)]
        t2 = [A.tile([128, 512], F32, "t2") for _ in range(2)]
        krf = A.tile([128, 512], F32, "krf")
        MS("pool", Vt.ap, 1.0, [Vt])
        R = slice(64, 96)

        def sub(tl):
            return T(tl.ap[R, :], tl.b)

        def ld_x(t):
            DMA("sp", xt[t % 2].ap, x[t * 512:(t + 1) * 512, :].rearrange("(n p) d -> p n d", p=128), [], [xt[t % 2]], f"x{t % 2}")

        def rope_tab(t):
            c0 = t * 512
            ropeC = ropeCs[t % 2]; ropeS = ropeSs[t % 2]
            DMA("sp", posi.ap[R, :], pos[:, c0:c0 + 512].partition_broadcast(32)[:, 0, :], [], [posi], "posi")
            CP("dve", angS.ap[R, :], posi.ap[R, :], [posi], [angS])
            TS("dve", angS.ap[R, :], angS.ap[R, :], ropec.ap[R, 0:1], None, ALU.mult, None, [angS, ropec], [angS])
            rr("dve", sub(angS), sub(rkf), sub(rki), [])
            ACT(ropeS.ap[R, :], angS.ap[R, :], AF.Sin, [angS, ropec], [ropeS], scale=ropec.ap[R, 1:2])
            STT("dve", angC.ap[R, :], angS.ap[R, :], -1.0, angS.ap[R, :], ALU.mult, ALU.max, [angS], [angC])
            ACT(ropeC.ap[R, :], angC.ap[R, :], AF.Sin, [angC, ropec], [ropeC], scale=-1.0, bias=ropec.ap[R, 2:3])

        ld_x(0)
        rope_tab(0)
        ln4(xt[0], g_in, b_in, hb)
        for t in range(NT):
            c0 = t * 512
            xs = xt[t % 2]
            ropeC = ropeCs[t % 2]; ropeS = ropeSs[t % 2]
            if t + 1 < NT:
                ld_x(t + 1)
            transpose_blocks(hb, hT)
            DMA("sp", H0[c0:c0 + 512, :].rearrange("(n p) d -> p n d", p=128), xs.ap, [xs], [], "stH0")
            DMA("sp", H0T[:, c0:c0 + 512].rearrange("(k p) s -> p k s", p=128), hT.ap, [hT], [], "stH0T")
            cps = None
            for i in range(4):
                ps = psum()
                for kc in range(8):
                    PE(ps.ap, WA.ap[:, kc, 256 + i * 128:256 + (i + 1) * 128], hT.ap[:, kc, :], kc == 0, kc == 7, [WA, hT.bufs[kc]], [ps])
                CP("act", cT[i].ap, ps.ap, [ps], [cT[i]])
            psms = []
            for j in range(2):
                psm = psum()
                psms.append(psm)
                for i in range(2):
                    sqt = sq[(2 * j + i) % len(sq)]
                    ACT(sqt.ap, cT[2 * j + i].ap, AF.Square, [cT[2 * j + i]], [sqt])
                    PE(psm.ap, ones_f.ap, sqt.ap, i == 0, i == 1, [ones_f, sqt], [psm])
            for j in range(2):
                ACT(rstd[j].ap, psms[j].ap, AF.Ln, [psms[j], epsc], [rstd[j]], scale=1.0 / 256.0, bias=epsc.ap[:, 0:1])
            for j in range(2):
                ACT(rstd[j].ap, rstd[j].ap, AF.Exp, [rstd[j]], [rstd[j]], scale=-0.5)
            for j in range(2):
                gcol = qg if j == 0 else kvg
                for i in range(2):
                    STT("dve", cn[j].ap[:, i, :], cT[2 * j + i].ap, gcol.ap[:, i:i + 1], rstd[j].ap,
                        ALU.mult, ALU.mult, [cT[2 * j + i], gcol, rstd[j]], [cn[j]])
            pa = psum(); pb = psum()
            for kc in range(8):
                PE(pa.ap[0:96, :], WA.ap[:, kc, 768:864], hT.ap[:, kc, :], kc == 0, kc == 7, [WA, hT.bufs[kc]], [pa])
            for kc in range(8):
                PE(pb.ap[0:96, :], WA.ap[:, kc, 864:960], hT.ap[:, kc, :], kc == 0, kc == 7, [WA, hT.bufs[kc]], [pb])
            TT("dve", t1[0].ap[R, :], pa.ap[R, :], ropeC.ap[R, :], ALU.mult, [pa, ropeC], [t1[0]])
            TT("dve", t2[0].ap[R, :], pb.ap[R, :], ropeS.ap[R, :], ALU.mult, [pb, ropeS], [t2[0]])
            TT("pool", krf.ap[R, :], t1[0].ap[R, :], t2[0].ap[R, :], ALU.add, [t1[0], t2[0]], [krf])
            CP("act", Kt.ap[R, :, :], krf.ap[R, :].unsqueeze(1).to_broadcast([32, 8, 512]), [krf], [Kt.bufs[8]])
            for tp in range(4):
                ps = psum()
                for tl in range(2):
                    tau = tp * 2 + tl
                    for kc in range(8):
                        PE(ps.ap[0:64, tl * 256:(tl + 1) * 256], hT.ap[:, kc, tau:512:8], WA.ap[:, kc, 0:256], kc == 0, kc == 7, [hT.bufs[kc], WA], [ps])
                CP("act" if tp % 2 else "dve", Uc.ap[0:64, :, tp * 2:tp * 2 + 2, :], ps.ap[0:64, :].rearrange("p (t g h) -> p g t h", t=2, g=16), [ps], [Uc.bufs[tp]])
            blk = c0 // 1024
            for gh in range(2):
                ps = psum()
                pv = bfv(ps)
                for g8 in range(8):
                    g = gh * 8 + g8
                    PET(pv[:, g8 * 64:(g8 + 1) * 64], Uc.ap[0:64, g, :, :].rearrange("p a b -> p (a b)"), ident_b.ap[0:64, 0:64], [Uc, ident_b], [ps])
                CP("act" if gh else "dve", Ug_all.ap[:, gh * 8:(gh + 1) * 8, t * 64:(t + 1) * 64],
                   pv[:, 0:512].rearrange("p (g c) -> p g c", g=8), [ps], [Ug_b[gh * 8 + g8][blk] for g8 in range(8)])
            if t + 1 < NT:
                cps = ln4(xt[(t + 1) % 2], g_in, b_in, hb, defer=True)
            for h in range(8):
                pa = psum(); pb = psum()
                for kc in range(2):
                    PE(pa.ap[0:96, :], Wuq.ap[:, kc, h * 96:(h + 1) * 96], cn[0].ap[:, kc, :], kc == 0, kc == 1, [Wuq, cn[0]], [pa])
                for kc in range(2):
                    PE(pb.ap[0:96, :], Wuqs.ap[:, kc, h * 96:(h + 1) * 96], cn[0].ap[:, kc, :], kc == 0, kc == 1, [Wuqs, cn[0]], [pb])
                CP("act", Qt.ap[0:64, h, :], pa.ap[0:64, :], [pa], [Qt.bufs[h]])
                TT("dve", t1[h % 2].ap[R, :], pa.ap[R, :], ropeC.ap[R, :], ALU.mult, [pa, ropeC], [t1[h % 2]])
                TT("dve", t2[h % 2].ap[R, :], pb.ap[R, :], ropeS.ap[R, :], ALU.mult, [pb, ropeS], [t2[h % 2]])
                TT("pool", Qt.ap[R, h, :], t1[h % 2].ap[R, :], t2[h % 2].ap[R, :], ALU.add, [t1[h % 2], t2[h % 2]], [Qt.bufs[8 + h]])
            for h in range(8):
                ps = psum()
                for kc in range(2):
                    PE(ps.ap[0:64, :], Wk.ap[:, kc, h * 64:(h + 1) * 64], cn[1].ap[:, kc, :], kc == 0, kc == 1, [Wk, cn[1]], [ps])
                CP("act" if h % 2 else "dve", Kt.ap[0:64, h, :], ps.ap[0:64, :], [ps], [Kt.bufs[h]])
            for n in range(4):
                ps = psum()
                for kc in range(2):
                    PE(ps.ap, cn[1].ap[:, kc, n * 128:(n + 1) * 128], Wv.ap[:, kc, :], kc == 0, kc == 1, [cn[1], Wv], [ps])
                CP("act" if n % 2 else "dve", Vt.ap[:, n, :, 0:64], ps.ap.rearrange("p (h c) -> p h c", h=8), [ps], [Vt.bufs[n]])
            if cps is not None:
                cps()
            if t + 1 < NT:
                rope_tab(t + 1)
            DMA("sp", Qs[:, :, c0:c0 + 512].rearrange("h r s -> r h s"), Qt.ap[0:96, :, :], [Qt], [], "stQ")
            DMA("sp", Ks[:, :, c0:c0 + 512].rearrange("h r s -> r h s"), Kt.ap[0:96, :, :], [Kt], [], "stK")
            for h in range(8):
                DMA("sp", Vs[h, :, t * 4:(t + 1) * 4, :], Vt.ap[:, :, h, :], [Vt], [], "stV")
        S_.barrier()

    if "S" in phases:
        A.top = base_top
        P64 = slice(0, 64)
        BmR = A.tile([128, 16, 64], BF16, "BmR"); BmI = A.tile([128, 16, 64], BF16, "BmI")
        Tg = A.tile([128, 16, 128], BF16, "Tg")
        CmR = A.tile([64, 16, 128], BF16, "CmR"); CmI = A.tile([64, 16, 128], BF16, "CmI")
        ER = A.tile([64, 16, 128], F32, "ER"); EI = A.tile([64, 16, 128], F32, "EI")
        Rt = A.tile([64, 16, 128], F32, "Rt")
        c1s = A.tile([64, 16], F32, "c1s"); s1s = A.tile([64, 16], F32, "s1s")
        dcols = A.tile([128, 16], F32, "dcols")
        s_top = A.top
        lr = A.tile([64, 16], F32, "lr"); li = A.tile([64, 16], F32, "li"); dtb = A.tile([64, 16], F32, "dtb")
        aa = A.tile([64, 16], F32, "aa"); ang = A.tile([64, 16], F32, "ang")
        kvec = A.tile([64, 16], F32, "kvec"); cvec = A.tile([64, 128], F32, "cvec")
        DMA("sp", lr.ap, lam_re.rearrange("g p -> p g"), [], [lr], "s5a", nonc=True)
        DMA("sp", li.ap, lam_im.rearrange("g p -> p g"), [], [li], "s5b", nonc=True)
        DMA("sp", dtb.ap, log_dt.partition_broadcast(64)[:, 0, :], [], [dtb], "s5c")
        DMA("sp", kvec.ap, c_kvec, [], [kvec], "s5d")
        DMA("sp", cvec.ap, c_cvec, [], [cvec], "s5e")
        for s in range(8):
            DMA("sp", dcols.ap[s * 16:(s + 1) * 16, :], s5_d.rearrange("g h -> h g"), [], [dcols], "s5f", nonc=True)
        Bre = A.tile([64, 16, 16], F32, "Bre"); Bim = A.tile([64, 16, 16], F32, "Bim")
        Cre = A.tile([64, 16, 16], F32, "Cre"); Cim = A.tile([64, 16, 16], F32, "Cim")
        DMA("sp", Bre.ap, b_re.rearrange("g p h -> p g h"), [], [Bre], "s5g", nonc=True)
        DMA("sp", Bim.ap, b_im.rearrange("g p h -> p g h"), [], [Bim], "s5h", nonc=True)
        for g in range(16):
            DMA("sp", Cre.ap[:, g, :], c_re[g].rearrange("h p -> p h"), [], [Cre], "s5i", nonc=True)
            DMA("sp", Cim.ap[:, g, :], c_im[g].rearrange("h p -> p h"), [], [Cim], "s5j", nonc=True)
        TS("dve", lr.ap, lr.ap, -1e-4, None, ALU.min, None, [lr], [lr])
        ACT(dtb.ap, dtb.ap, AF.Exp, [dtb], [dtb])
        TT("dve", aa.ap, lr.ap, dtb.ap, ALU.mult, [lr, dtb], [aa])
        TT("dve", ang.ap, li.ap, dtb.ap, ALU.mult, [li, dtb], [ang])
        pwm = A.tile([64, 16, 16], F32, "pwm"); pwa = A.tile([64, 16, 16], F32, "pwa"); pwb = A.tile([64, 16, 16], F32, "pwb")
        pwr = A.tile([64, 16, 16], F32, "pwr"); pwi = A.tile([64, 16, 16], F32, "pwi")
        kfs = A.tile([64, 2048], F32, "kfs"); kis = A.tile([64, 2048], I32, "kis")

        def bc_g(tl):
            return tl.ap.unsqueeze(2).to_broadcast([64, 16, 16])

        def bc_k(tl, n=16):
            return tl.ap.unsqueeze(1).to_broadcast([64, 16, n])

        TT("dve", pwm.ap, bc_g(aa), bc_k(kvec), ALU.mult, [aa, kvec], [pwm])
        ACT(pwm.ap, pwm.ap, AF.Exp, [pwm], [pwm])
        TT("dve", pwa.ap, bc_g(ang), bc_k(kvec), ALU.mult, [ang, kvec], [pwa])
        TS("dve", pwb.ap, pwa.ap, PI / 2, None, ALU.add, None, [pwa], [pwb])

        def flat(tl, n):
            return T(tl.ap.rearrange("p a b -> p (a b)"), tl.b)

        kf256 = T(kfs.ap[:, 0:256], kfs.b); ki256 = T(kis.ap[:, 0:256], kis.b)
        rr("dve", flat(pwa, 256), kf256, ki256, [])
        rr("dve", flat(pwb, 256), kf256, ki256, [])
        ACT(pwa.ap, pwa.ap, AF.Sin, [pwa], [pwa])
        ACT(pwb.ap, pwb.ap, AF.Sin, [pwb], [pwb])
        TT("dve", pwr.ap, pwm.ap, pwb.ap, ALU.mult, [pwm, pwb], [pwr])
        TT("dve", pwi.ap, pwm.ap, pwa.ap, ALU.mult, [pwm, pwa], [pwi])
        den = A.tile([64, 16], F32, "den"); nr = A.tile([64, 16], F32, "nr")
        fr = A.tile([64, 16], F32, "fr"); fi = A.tile([64, 16], F32, "fi"); tq = A.tile([64, 16], F32, "tq")
        lbr = pwr.ap[:, :, 8]; lbi = pwi.ap[:, :, 8]
        TT("dve", den.ap, lr.ap, lr.ap, ALU.mult, [lr], [den])
        TT("dve", tq.ap, li.ap, li.ap, ALU.mult, [li], [tq])
        TT("dve", den.ap, den.ap, tq.ap, ALU.add, [den, tq], [den])
        S_.op("dve", lambda e: e.reciprocal(den.ap, den.ap), [den.b], [den.b])
        TS("dve", nr.ap, lbr, -1.0, None, ALU.add, None, [pwr], [nr])
        TT("dve", fr.ap, nr.ap, lr.ap, ALU.mult, [nr, lr], [fr])
        TT("dve", tq.ap, lbi, li.ap, ALU.mult, [pwi, li], [tq])
        TT("dve", fr.ap, fr.ap, tq.ap, ALU.add, [fr, tq], [fr])
        TT("dve", fr.ap, fr.ap, den.ap, ALU.mult, [fr, den], [fr])
        TT("dve", fi.ap, lbi, lr.ap, ALU.mult, [pwi, lr], [fi])
        TT("dve", tq.ap, nr.ap, li.ap, ALU.mult, [nr, li], [tq])
        TT("dve", fi.ap, fi.ap, tq.ap, ALU.subtract, [fi, tq], [fi])
        TT("dve", fi.ap, fi.ap, den.ap, ALU.mult, [fi, den], [fi])
        Bbr = A.tile([64, 16, 16], F32, "Bbr"); Bbi = A.tile([64, 16, 16], F32, "Bbi"); tb = A.tile([64, 16, 16], F32, "tb")
        TT("dve", Bbr.ap, Bre.ap, bc_g(fr), ALU.mult, [Bre, fr], [Bbr])
        TT("dve", tb.ap, Bim.ap, bc_g(fi), ALU.mult, [Bim, fi], [tb])
        TT("dve", Bbr.ap, Bbr.ap, tb.ap, ALU.subtract, [Bbr, tb], [Bbr])
        TT("dve", Bbi.ap, Bim.ap, bc_g(fr), ALU.mult, [Bim, fr], [Bbi])
        TT("dve", tb.ap, Bre.ap, bc_g(fi), ALU.mult, [Bre, fi], [tb])
        TT("dve", Bbi.ap, Bbi.ap, tb.ap, ALU.add, [Bbi, tb], [Bbi])
        XPr = A.tile([64, 16, 8, 16], F32, "XPr"); XPi = A.tile([64, 16, 8, 16], F32, "XPi")
        CPr = A.tile([64, 16, 16, 16], F32, "CPr"); CPn = A.tile([64, 16, 16, 16], F32, "CPn")
        tx = A.tile([64, 16, 16, 16], F32, "tx")

        def b4(tl, n):
            return tl.ap.unsqueeze(2).to_broadcast([64, 16, n, 16])

        for s in range(8):
            pr = pwr.ap[:, :, 14 - s:15 - s].to_broadcast([64, 16, 16])
            pi_ = pwi.ap[:, :, 14 - s:15 - s].to_broadcast([64, 16, 16])
            TT("dve", XPr.ap[:, :, s, :], Bbr.ap, pr, ALU.mult, [Bbr, pwr], [XPr])
            TT("pool", tb.ap, Bbi.ap, pi_, ALU.mult, [Bbi, pwi], [tb])
            TT("dve", XPr.ap[:, :, s, :], XPr.ap[:, :, s, :], tb.ap, ALU.subtract, [XPr, tb], [XPr])
            TT("dve", XPi.ap[:, :, s, :], Bbr.ap, pi_, ALU.mult, [Bbr, pwi], [XPi])
            TT("pool", tb.ap, Bbi.ap, pr, ALU.mult, [Bbi, pwr], [tb])
            TT("dve", XPi.ap[:, :, s, :], XPi.ap[:, :, s, :], tb.ap, ALU.add, [XPi, tb], [XPi])
        pr4 = pwr.ap.unsqueeze(3).to_broadcast([64, 16, 16, 16])
        pi4 = pwi.ap.unsqueeze(3).to_broadcast([64, 16, 16, 16])
        TT("dve", CPr.ap, b4(Cre, 16), pr4, ALU.mult, [Cre, pwr], [CPr])
        TT("dve", tx.ap, b4(Cim, 16), pi4, ALU.mult, [Cim, pwi], [tx])
        TT("dve", CPr.ap, CPr.ap, tx.ap, ALU.subtract, [CPr, tx], [CPr])
        TT("dve", CPn.ap, b4(Cre, 16), pi4, ALU.mult, [Cre, pwi], [CPn])
        TT("dve", tx.ap, b4(Cim, 16), pr4, ALU.mult, [Cim, pwr], [tx])
        STT("dve", CPn.ap, CPn.ap, -1.0, tx.ap, ALU.mult, ALU.subtract, [CPn, tx], [CPn])
        for g in range(16):
            CP("act", CmR.ap[:, g, :], CPr.ap[:, g, 8:16, :].rearrange("p a b -> p (a b)"), [CPr], [CmR])
            CP("pool", CmI.ap[:, g, :], CPn.ap[:, g, 8:16, :].rearrange("p a b -> p (a b)"), [CPn], [CmI])
        ttmp = A.tile([128, 128], F32, "ttmp")
        for g in range(16):
            xr = XPr.ap[:, g, :, :].rearrange("p a b -> p (a b)")
            xi = XPi.ap[:, g, :, :].rearrange("p a b -> p (a b)")
            ps = psum()
            PE(ps.ap[:, 0:64], xr, ident_f.ap[0:64, 0:64], True, True, [XPr, ident_f], [ps])
            PE(ps.ap[:, 64:128], xi, ident_f.ap[0:64, 0:64], True, True, [XPi, ident_f], [ps])
            CP("act", BmR.ap[:, g, :], ps.ap[:, 0:64], [ps], [BmR])
            CP("act", BmI.ap[:, g, :], ps.ap[:, 64:128], [ps], [BmI])
            ps2 = psum()
            yr = CPr.ap[:, g, 0:8, :].rearrange("p a b -> p (a b)")
            yn = CPn.ap[:, g, 0:8, :].rearrange("p a b -> p (a b)")
            PE(ps2.ap[:, 0:128], xr, yr, True, False, [XPr, CPr], [ps2])
            PE(ps2.ap[:, 0:128], xi, yn, False, True, [XPi, CPn], [ps2])
            TT("dve", ttmp.ap, ps2.ap[:, 0:128], cm_f.ap, ALU.mult, [ps2, cm_f], [ttmp])
            STT("dve", Tg.ap[:, g, :], ident_f.ap, dcols.ap[:, g:g + 1], ttmp.ap, ALU.mult, ALU.add, [ident_f, dcols, ttmp], [Tg])
        th = A.tile([64, 16], F32, "th"); r8 = A.tile([64, 16], F32, "r8")
        TS("dve", th.ap, ang.ap, 8.0, None, ALU.mult, None, [ang], [th])
        rr("dve", th, T(kfs.ap[:, 0:16], kfs.b), T(kis.ap[:, 0:16], kis.b), [])
        TS("dve", r8.ap, aa.ap, 8.0, None, ALU.mult, None, [aa], [r8])
        ACT(r8.ap, r8.ap, AF.Exp, [r8], [r8])
        CP("dve", Rt.ap, r8.ap.unsqueeze(2).to_broadcast([64, 16, 128]), [r8], [Rt])
        TT("dve", EI.ap, th.ap.unsqueeze(2).to_broadcast([64, 16, 128]), cvec.ap.unsqueeze(1).to_broadcast([64, 16, 128]),
           ALU.mult, [th, cvec], [EI])
        TS("dve", ER.ap, EI.ap, PI / 2, None, ALU.add, None, [EI], [ER])
        rr("dve", flat(EI, 2048), kfs, kis, [])
        rr("dve", flat(ER, 2048), kfs, kis, [])
        ACT(EI.ap, EI.ap, AF.Sin, [EI], [EI])
        ACT(ER.ap, ER.ap, AF.Sin, [ER], [ER])
        CP("dve", c1s.ap, ER.ap[:, :, 1], [ER], [c1s])
        CP("dve", s1s.ap, EI.ap[:, :, 1], [EI], [s1s])
        S_.barrier()
        A.top = s_top
        Hp = [[A.tile([64, 8, 129], BF16, "HpR") for _ in range(2)], [A.tile([64, 8, 129], BF16, "HpI") for _ in range(2)]]
        Gi = [[A.tile([64, 8], F32, "Gi") for _ in range(2)] for _ in range(2)]
        Hl = [[A.tile([64, 8], F32, "Hl") for _ in range(2)] for _ in range(2)]
        EB = [{k: A.tile([64, 8, 128], F32, k) for k in ("e1", "e2", "e3", "e4", "BpR", "BpI", "GR", "GI")} for _ in range(2)]
        Ygs = [A.tile([128, 8, 128], BF16, "Yg") for _ in range(2)]
        Yc = A.tile([128, 8, 256], BF16, "Yc")
        yTb = A.tile([128, 2, 1024], BF16, "yTb")
        hls = [[A.tile([64, 8], F32, "hl") for _ in range(2)] for _ in range(2)]
        Wxk = A.tile([128, 8, 1024], BF16, "Wxk"); Wxv = A.tile([128, 8, 1024], BF16, "Wxv")
        memb = A.tile([128, 2, 1024], BF16, "memb"); memT = A.tile([128, 8, 256], BF16, "memT")
        wload(Wxk.ap, w_xk.rearrange("(kc p) n -> p kc n", p=128), [Wxk], "wxk")
        wload(Wxv.ap, w_xv.rearrange("(kc p) n -> p kc n", p=128), [Wxv], "wxv")
        wload(memb.ap, mem.rearrange("(n p) d -> p n d", p=128), [memb], "memb")

        def mem_kv():
            for c in range(8):
                ps = psum(); pv = bfv(ps)
                for n in range(2):
                    PET(pv[:, n * 128:(n + 1) * 128], memb.ap[:, n, c * 128:(c + 1) * 128], ident_b.ap, [memb, ident_b], [ps])
                CP("act" if c % 2 else "dve", memT.ap[:, c, :], pv[:, 0:256], [ps], [memT])
            for c in range(8):
                ps = psum()
                for kc in range(8):
                    PE(ps.ap[:, 0:256], Wxk.ap[:, kc, c * 128:(c + 1) * 128], memT.ap[:, kc, :], kc == 0, kc == 7, [Wxk, memT], [ps])
                CP("act" if c % 2 else "dve", KmT.ap[:, c, :], ps.ap[:, 0:256], [ps], [KmT])
            for mb in range(2):
                for hf in range(2):
                    ps = psum()
                    for kc in range(8):
                        PE(ps.ap, memT.ap[:, kc, mb * 128:(mb + 1) * 128], Wxv.ap[:, kc, hf * 512:(hf + 1) * 512], kc == 0, kc == 7, [memT, Wxv], [ps])
                    CP("act" if hf else "dve", Vm.ap[:, mb, hf * 512:(hf + 1) * 512], ps.ap, [ps], [Vm])

        for gh in range(2):
            for ri in range(2):
                MS("dve", Gi[gh][ri].ap, 0.0, [Gi[gh][ri]])
                MS("pool", Hp[ri][gh].ap, 0.0, [Hp[ri][gh]])

        GRb = [[[Buf() for _ in range(8)] for _ in range(2)] for _ in range(2)]
        Ehb = [{k: [Buf(), Buf()] for k in ("e1", "e2", "e3", "e4")} for _ in range(2)]

        def s_stage1(blk, gh):
            cb = blk * 128
            B_ = EB[gh]
            pR = [psum(), psum()]; pI = [psum(), psum()]
            for g8 in range(8):
                g = gh * 8 + g8
                u_ap = Ug_all.ap[:, g, cb:cb + 128]
                PE(pR[g8 // 4].ap[0:64, (g8 % 4) * 128:(g8 % 4 + 1) * 128], BmR.ap[:, g, :], u_ap, True, True, [BmR, Ug_b[g][blk]], [pR[g8 // 4]])
                PE(pI[g8 // 4].ap[0:64, (g8 % 4) * 128:(g8 % 4 + 1) * 128], BmI.ap[:, g, :], u_ap, True, True, [BmI, Ug_b[g][blk]], [pI[g8 // 4]])
            for hf in range(2):
                hs = slice(hf * 4, hf * 4 + 4)
                gsl = slice(gh * 8 + hf * 4, gh * 8 + hf * 4 + 4)
                sr = pR[hf].ap[0:64, :].rearrange("p (a b) -> p a b", a=4)
                si = pI[hf].ap[0:64, :].rearrange("p (a b) -> p a b", a=4)
                TT("dve", B_["e1"].ap[:, hs, :], sr, ER.ap[:, gsl, :], ALU.mult, [pR[hf], ER], [Ehb[gh]["e1"][hf]])
                TT("dve", B_["e2"].ap[:, hs, :], si, EI.ap[:, gsl, :], ALU.mult, [pI[hf], EI], [Ehb[gh]["e2"][hf]])
                TT("dve", B_["e3"].ap[:, hs, :], si, ER.ap[:, gsl, :], ALU.mult, [pI[hf], ER], [Ehb[gh]["e3"][hf]])
                TT("dve", B_["e4"].ap[:, hs, :], sr, EI.ap[:, gsl, :], ALU.mult, [pR[hf], EI], [Ehb[gh]["e4"][hf]])
            TT("pool", B_["BpR"].ap, B_["e1"].ap, B_["e2"].ap, ALU.add, Ehb[gh]["e1"] + Ehb[gh]["e2"] + [B_["e1"], B_["e2"]], [B_["BpR"]])
            TT("pool", B_["BpI"].ap, B_["e3"].ap, B_["e4"].ap, ALU.subtract, Ehb[gh]["e3"] + Ehb[gh]["e4"] + [B_["e3"], B_["e4"]], [B_["BpI"]])

        def s_stage2(blk, gh):
            B_ = EB[gh]
            gs = slice(gh * 8, gh * 8 + 8)
            hpr = Hp[0][gh]; hpi = Hp[1][gh]
            GR = B_["GR"]; GI = B_["GI"]; HR = B_["e1"]; HI = B_["e3"]
            for g8 in range(8):
                g = gh * 8 + g8
                S_.op("dve", lambda e, g8=g8, g=g: e.tensor_tensor_scan(out=GR.ap[:, g8, :], data0=Rt.ap[:, g, :], data1=B_["BpR"].ap[:, g8, :],
                      initial=Gi[gh][0].ap[:, g8:g8 + 1], op0=ALU.mult, op1=ALU.add), [Rt.b, B_["BpR"].b, Gi[gh][0].b], [GRb[gh][0][g8]])
                S_.op("dve", lambda e, g8=g8, g=g: e.tensor_tensor_scan(out=GI.ap[:, g8, :], data0=Rt.ap[:, g, :], data1=B_["BpI"].ap[:, g8, :],
                      initial=Gi[gh][1].ap[:, g8:g8 + 1], op0=ALU.mult, op1=ALU.add), [Rt.b, B_["BpI"].b, Gi[gh][1].b], [GRb[gh][1][g8]])
            gr_ = GRb[gh][0]; gi_ = GRb[gh][1]
            TT("dve", B_["e1"].ap, GR.ap, ER.ap[:, gs, :], ALU.mult, gr_ + [ER], [B_["e1"]] + Ehb[gh]["e1"])
            TT("pool", B_["e2"].ap, GI.ap, EI.ap[:, gs, :], ALU.mult, gi_ + [EI], [B_["e2"]] + Ehb[gh]["e2"])
            TT("dve", B_["e3"].ap, GR.ap, EI.ap[:, gs, :], ALU.mult, gr_ + [EI], [B_["e3"]] + Ehb[gh]["e3"])
            TT("pool", B_["e4"].ap, GI.ap, ER.ap[:, gs, :], ALU.mult, gi_ + [ER], [B_["e4"]] + Ehb[gh]["e4"])
            TT("dve", HR.ap, B_["e1"].ap, B_["e2"].ap, ALU.subtract, [B_["e1"], B_["e2"]] + Ehb[gh]["e2"], [HR] + Ehb[gh]["e1"])
            TT("pool", HI.ap, B_["e3"].ap, B_["e4"].ap, ALU.add, [B_["e3"], B_["e4"]] + Ehb[gh]["e4"], [HI] + Ehb[gh]["e3"])
            hr_rd = [HR] + Ehb[gh]["e1"]; hi_rd = [HI] + Ehb[gh]["e3"]
            CP("act", hpr.ap[:, :, 1:129], HR.ap, hr_rd, [hpr])
            CP("act", hpi.ap[:, :, 1:129], HI.ap, hi_rd, [hpi])
            if blk + 1 < NBLK:
                hl1, hl2 = hls[gh]
                CP("dve", Hl[gh][0].ap, HR.ap[:, :, 127], hr_rd, [Hl[gh][0]])
                CP("dve", Hl[gh][1].ap, HI.ap[:, :, 127], hi_rd, [Hl[gh][1]])
                TT("dve", hl1.ap, Hl[gh][0].ap, c1s.ap[:, gs], ALU.mult, [Hl[gh][0], c1s], [hl1])
                TT("dve", hl2.ap, Hl[gh][1].ap, s1s.ap[:, gs], ALU.mult, [Hl[gh][1], s1s], [hl2])
                TT("dve", Gi[gh][0].ap, hl1.ap, hl2.ap, ALU.subtract, [hl1, hl2], [Gi[gh][0]])
                TT("dve", hl1.ap, Hl[gh][0].ap, s1s.ap[:, gs], ALU.mult, [Hl[gh][0], s1s], [hl1])
                TT("dve", hl2.ap, Hl[gh][1].ap, c1s.ap[:, gs], ALU.mult, [Hl[gh][1], c1s], [hl2])
                TT("dve", Gi[gh][1].ap, hl1.ap, hl2.ap, ALU.add, [hl1, hl2], [Gi[gh][1]])

        def s_stage3(blk, gh):
            cb = blk * 128
            hpr = Hp[0][gh]; hpi = Hp[1][gh]
            Yg = Ygs[gh]
            gs = slice(gh * 8, gh * 8 + 8)
            pY = [psum(), psum()]
            for g8 in range(8):
                g = gh * 8 + g8
                yo = pY[g8 // 4].ap[:, (g8 % 4) * 128:(g8 % 4 + 1) * 128]
                PE(yo, Tg.ap[:, g, :], Ug_all.ap[:, g, cb:cb + 128], True, False, [Tg, Ug_b[g][blk]], [pY[g8 // 4]])
                PE(yo, CmR.ap[:, g, :], hpr.ap[:, g8, 0:128], False, False, [CmR, hpr], [pY[g8 // 4]])
                PE(yo, CmI.ap[:, g, :], hpi.ap[:, g8, 0:128], False, True, [CmI, hpi], [pY[g8 // 4]])
            if blk + 1 < NBLK:
                CP("act", hpr.ap[:, :, 0], Hl[gh][0].ap, [Hl[gh][0]], [hpr])
                CP("act", hpi.ap[:, :, 0], Hl[gh][1].ap, [Hl[gh][1]], [hpi])
            for hf in range(2):
                ACT(Yg.ap[:, hf * 4:hf * 4 + 4, :], pY[hf].ap.rearrange("p (a b) -> p a b", a=4), AF.Gelu, [pY[hf]], [Yg])
            for hf in range(2):
                ps = psum()
                pv = bfv(ps)
                for q4 in range(4):
                    PET(pv[:, q4 * 128:(q4 + 1) * 128], Yg.ap[:, hf * 4 + q4, :], ident_b.ap, [Yg, ident_b], [ps])
                ch0 = gh * 128 + hf * 64
                CP("dve" if hf else "act", Yc.ap[:, :, ch0:ch0 + 64].rearrange("p t (g h) -> p g t h", g=4),
                   pv[:, 0:512].rearrange("p (g t h) -> p g t h", g=4, t=8), [ps], [Ycb[gh][hf]])

        Ycb = [[Buf(), Buf()], [Buf(), Buf()]]
        for blk in range(NBLK):
            for gh in range(2):
                s_stage1(blk, gh)
            for gh in range(2):
                s_stage2(blk, gh)
            for gh in range(2):
                s_stage3(blk, gh)
            for chn in range(2):
                for th_ in range(2):
                    ps = psum()
                    pv = bfv(ps)
                    for q4 in range(4):
                        tau = th_ * 4 + q4
                        PET(pv[:, q4 * 128:(q4 + 1) * 128], Yc.ap[:, tau, chn * 128:(chn + 1) * 128], ident_b.ap, Ycb[chn] + [ident_b], [ps])
                    CP("act" if th_ else "dve", yTb.ap[:, chn, :].rearrange("p (c t) -> p t c", t=8)[:, th_ * 4:th_ * 4 + 4, :],
                       pv[:, 0:512].rearrange("p (t c) -> p t c", t=4), [ps], [yTb])
            DMA("sp", Ys[:, blk * 1024:(blk + 1) * 1024].rearrange("(k p) s -> p k s", p=128), yTb.ap, [yTb], [], "stY")
            if blk == min(1, NBLK - 1):
                mem_kv()
        S_.barrier()

    CW = {}

    def alloc_c_weights():
        A.top = base0
        CW["WG"] = A.tile([128, 8, 2048], BF16, "WG")
        CW["Wgl"] = A.tile([128, 2, 2048], BF16, "Wgl")
        CW["Woa"] = A.tile([128, 4, 1024], BF16, "Woa")
        CW["Wo"] = A.tile([128, 8, 1024], BF16, "Wo")
        CW["Wxq"] = A.tile([128, 8, 1024], BF16, "Wxq")
        CW["Wxo"] = A.tile([128, 8, 1024], BF16, "Wxo")
        lnp = {}
        for nm, (gg, bb) in {"1": (ln1_g, ln1_b), "2": (ln2_g, ln2_b)}.items():
            lnp[nm] = (A.tile([128, 1024], F32, "g" + nm), A.tile([128, 1024], F32, "b" + nm))
            load_bc(lnp[nm][0], gg); load_bc(lnp[nm][1], bb)
        CW["lnp"] = lnp
        wload(CW["WG"].ap, w_in.rearrange("(kc p) n -> p kc n", p=128)[:, :, 800:2848], [CW["WG"]], "wG")
        wload(CW["Wgl"].ap, w_glu.rearrange("(kc p) n -> p kc n", p=128), [CW["Wgl"]], "wgl")
        wload(CW["Woa"].ap, w_oa.rearrange("(kc p) n -> p kc n", p=128), [CW["Woa"]], "woa")
        wload(CW["Wo"].ap, w_o.rearrange("(kc p) n -> p kc n", p=128), [CW["Wo"]], "wo")
        wload(CW["Wxq"].ap, w_xq.rearrange("(kc p) n -> p kc n", p=128), [CW["Wxq"]], "wxq")
        wload(CW["Wxo"].ap, w_xo.rearrange("(kc p) n -> p kc n", p=128), [CW["Wxo"]], "wxo")
        CW["top"] = A.top

    if "B" in phases:
        alloc_c_weights()
        Kh = [A.tile([128, S], BF16, "Kh") for _ in range(2)]
        Vh = [A.tile([128, NB, 128], BF16, "Vh") for _ in range(2)]
        Qh = [A.tile([128, 512], BF16, "Qh") for _ in range(3)]
        rd = [A.tile([128, 512], F32, "rd") for _ in range(2)]
        On = [A.tile([128, 512], BF16, "On") for _ in range(2)]
        scale = 96.0 ** -0.5
        LA = 2
        NSP = 3
        PS_O = [PSB[6], PSB[7]]
        for k_ in range(2):
            MS("pool", Vh[k_].ap[:, :, 64:128], 1.0, [Vh[k_]])
        NPP = 4
        Pp = [A.tile([128, 2, 512], BF16, "Pp") for _ in range(NPP)]
        ptb = [[Buf(), Buf()] for _ in range(NPP)]

        def ld_kv(h):
            DMA("sp", Kh[h % 2].ap[0:96, :], Ks[h], [], [Kh[h % 2]], f"ldK{h % 2}")
            DMA("sp", Vh[h % 2].ap[:, :, 0:65], Vs[h], [], [Vh[h % 2]], f"ldV{h % 2}")
        ld_kv(0)
        qlist = [(h, j) for h in range(8) for j in range(NT)]

        def ld_q(qq):
            if qq < len(qlist):
                hh, jj = qlist[qq]
                DMA("sp", Qh[qq % 3].ap[0:96, :], Qs[hh, :, jj * 512:(jj + 1) * 512], [], [Qh[qq % 3]], f"ldQ{qq % 3}")

        ld_q(0); ld_q(1)
        its = []
        for qq, (h, j) in enumerate(qlist):
            npair = 2 * j + 2
            for pp in range(npair):
                its.append((qq, h, j, pp, npair))
        N = len(its)
        pend = []
        for n in range(N + LA):
            if n < N:
                qq, h, j, pp, npair = its[n]
                kh = Kh[h % 2]
                if pp == 0:
                    ld_q(qq + 2)
                    if j == min(1, NT - 1) and h + 1 < 8:
                        ld_kv(h + 1)
                qh = Qh[qq % 3]
                kb0 = 2 * pp
                i0_ = kb0 - 4 * j
                lo = 128 * i0_ if i0_ > 0 else 0
                sp_ = PSP[n % NSP]
                sb0 = PSB[2 * (n % NSP)]; sb1 = PSB[2 * (n % NSP) + 1]
                pt = Pp[n % NPP]
                PE(sp_[:, lo:512], kh.ap[0:96, kb0 * 128:(kb0 + 1) * 128], qh.ap[0:96, lo:512], True, True, [kh, qh], [sb0])
                PE(sp_[:, 512 + lo:1024], kh.ap[0:96, (kb0 + 1) * 128:(kb0 + 2) * 128], qh.ap[0:96, lo:512], True, True, [kh, qh], [sb1])
                ACT(pt.ap[:, :, lo:512], sp_[:, :].rearrange("p (a b) -> p a b", a=2)[:, :, lo:512], AF.Exp, [sb0, sb1], [pt] + ptb[n % NPP], scale=scale)
                if i0_ >= 0:
                    TT("pool", pt.ap[:, 0, lo:lo + 128], pt.ap[:, 0, lo:lo + 128], tri_b.ap, ALU.mult, [pt, tri_b], [ptb[n % NPP][0]])
                    TT("pool", pt.ap[:, 1, lo + 128:lo + 256], pt.ap[:, 1, lo + 128:lo + 256], tri_b.ap, ALU.mult, [pt, tri_b], [ptb[n % NPP][1]])
            m = n - LA
            if m >= 0:
                qq, h, j, pp, npair = its[m]
                vh = Vh[h % 2]
                kb0 = 2 * pp
                i0_ = kb0 - 4 * j
                lo_a = 128 * i0_ if i0_ > 0 else 0
                lo_b = lo_a + 128 if i0_ >= 0 else 0
                po = PS_O[qq % 2]
                pt = Pp[m % NPP]
                PE(po.ap[:, lo_a:512], vh.ap[:, kb0, :], pt.ap[:, 0, lo_a:512], pp == 0, False, [vh, pt, ptb[m % NPP][0]], [po])
                PE(po.ap[:, lo_b:512], vh.ap[:, kb0 + 1, :], pt.ap[:, 1, lo_b:512], False, pp == npair - 1, [vh, pt, ptb[m % NPP][1]], [po])
                if pp == npair - 1:
                    rdt = rd[qq % 2]; on = On[qq % 2]
                    S_.op("dve", lambda e, r=rdt, p_=po: e.reciprocal(r.ap[64:128, :], p_.ap[64:128, :]), [po.b], [rdt.b])
                    TT("dve", on.ap[0:64, :], po.ap[0:64, :], rdt.ap[64:128, :], ALU.mult, [po, rdt], [on])
                    DMA("pool", Os[h * 64:(h + 1) * 64, j * 512:(j + 1) * 512], on.ap[0:64, :], [on], [], f"stO{qq % 2}")
        S_.barrier()

    if "C" in phases:
        if not CW:
            alloc_c_weights()
        WG = CW["WG"]; Wgl = CW["Wgl"]; Woa = CW["Woa"]; Wo = CW["Wo"]; Wxq = CW["Wxq"]; Wxo = CW["Wxo"]; lnp = CW["lnp"]
        A.top = CW["top"]
        ln_alloc(1)
        xs = MT(A.tile([128, 4, 1024], F32, "xs"))
        hb = MT(A.tile([128, 4, 1024], BF16, "hb"))
        hT = MT(A.tile([128, 8, 512], BF16, "hT"), 8)
        h0Ts = [A.tile([128, 8, 512], BF16, "h0T")] * 2
        mixins = [MT(A.tile([128, 8, 512], BF16, "mixin"), 8) for _ in range(2)]
        qx_ap = hb.ap.rearrange("p n (h s) -> p (n h) s", h=2)
        OTs = [A.tile([128, 4, 512], BF16, "OT") for _ in range(2)]
        yTts = [A.tile([128, 2, 512], BF16, "yTt") for _ in range(2)]
        gsb = [A.tile([128, 512], BF16, "gs") for _ in range(2)]
        gab = [A.tile([128, 512], BF16, "ga")] * 2
        sgb = [A.tile([128, 512], BF16, "sg")] * 2
        u1 = [A.tile([128, 512], F32, "u1")] * 2
        u3 = [A.tile([128, 512], F32, "u3")] * 2
        PTx = [A.tile([128, 512], BF16, "PTx") for _ in range(2)]
        rdx = A.tile([128, 512], F32, "rdx")

        def ld_c_small(t):
            c0 = t * 512
            DMA("sp", h0Ts[t % 2].ap, H0T[:, c0:c0 + 512].rearrange("(k p) s -> p k s", p=128), [], [h0Ts[t % 2]], "ldH0T")
            DMA("sp", OTs[t % 2].ap, Os[:, c0:c0 + 512].rearrange("(k p) s -> p k s", p=128), [], [OTs[t % 2]], f"ldO{t % 2}")
            DMA("sp", yTts[t % 2].ap, Ys[:, c0:c0 + 512].rearrange("(k p) s -> p k s", p=128), [], [yTts[t % 2]], f"ldY{t % 2}")

        def ld_c_x(t):
            DMA("sp", xs.ap, H0[t * 512:(t + 1) * 512, :].rearrange("(n p) d -> p n d", p=128), [], [xs], "xC")

        def chunk_loop(t, crange):
            h0T = h0Ts[t % 2]; OT = OTs[t % 2]; yTt = yTts[t % 2]; mixin = mixins[t % 2]
            for c in crange:
                k = c % 2
                pgs = psum(); pga = psum(); pA = psum(); pB = psum(); pO = psum()
                for kc in range(8):
                    PE(pgs.ap, WG.ap[:, kc, c * 128:(c + 1) * 128], h0T.ap[:, kc, :], kc == 0, kc == 7, [WG, h0T], [pgs])
                for kc in range(8):
                    PE(pga.ap, WG.ap[:, kc, 1024 + c * 128:1024 + (c + 1) * 128], h0T.ap[:, kc, :], kc == 0, kc == 7, [WG, h0T], [pga])
                for kc in range(2):
                    PE(pA.ap, Wgl.ap[:, kc, c * 128:(c + 1) * 128], yTt.ap[:, kc, :], kc == 0, kc == 1, [Wgl, yTt], [pA])
                for kc in range(2):
                    PE(pB.ap, Wgl.ap[:, kc, 1024 + c * 128:1024 + (c + 1) * 128], yTt.ap[:, kc, :], kc == 0, kc == 1, [Wgl, yTt], [pB])
                for kc in range(4):
                    PE(pO.ap, Woa.ap[:, kc, c * 128:(c + 1) * 128], OT.ap[:, kc, :], kc == 0, kc == 3, [Woa, OT], [pO])
                ACT(gsb[k].ap, pgs.ap, AF.Sigmoid, [pgs], [gsb[k]])
                ACT(gab[k].ap, pga.ap, AF.Sigmoid, [pga], [gab[k]])
                ACT(sgb[k].ap, pB.ap, AF.Sigmoid, [pB], [sgb[k]])
                TT("dve", u1[k].ap, pA.ap, sgb[k].ap, ALU.mult, [pA, sgb[k]], [u1[k]])
                TT("dve", u1[k].ap, u1[k].ap, gsb[k].ap, ALU.mult, [u1[k], gsb[k]], [u1[k]])
                TT("dve", u3[k].ap, pO.ap, gab[k].ap, ALU.mult, [pO, gab[k]], [u3[k]])
                TT("pool", mixin.ap[:, c, :], u1[k].ap, u3[k].ap, ALU.add, [u1[k], u3[k]], [mixin.bufs[c]])

        def wo_ln1(t):
            mixin = mixins[t % 2]
            for n in range(4):
                for hf in range(2):
                    ps = psum()
                    for kc in range(8):
                        PE(ps.ap, mixin.ap[:, kc, n * 128:(n + 1) * 128], Wo.ap[:, kc, hf * 512:(hf + 1) * 512], kc == 0, kc == 7, [mixin, Wo], [ps])
                    STT("dve", xs.ap[:, n, hf * 512:(hf + 1) * 512], xs.ap[:, n, hf * 512:(hf + 1) * 512], DN_ALPHA, ps.ap, ALU.mult, ALU.add, [xs.bufs[n], ps], [xs.bufs[n]])

        def xattn(t):
            c0 = t * 512
            mixin = mixins[t % 2]
            transpose_blocks(hb, hT)
            for c in range(8):
                ps = psum()
                for kc in range(8):
                    PE(ps.ap, Wxq.ap[:, kc, c * 128:(c + 1) * 128], hT.ap[:, kc, :], kc == 0, kc == 7, [Wxq, hT.bufs[kc]], [ps])
                CP("act" if c % 2 else "dve", qx_ap[:, c, :], ps.ap, [ps], [hb.bufs[c // 2]])
            for hx in range(4):
                qb = [hb.bufs[hx]]
                for mb in range(2):
                    ps = psum()
                    for dc in range(2):
                        PE(ps.ap, KmT.ap[:, 2 * hx + dc, mb * 128:(mb + 1) * 128], qx_ap[:, 2 * hx + dc, :], dc == 0, dc == 1, [KmT] + qb, [ps])
                    ACT(PTx[mb].ap, ps.ap, AF.Exp, [ps], [PTx[mb]], scale=1.0 / 16.0)
                pd = psum()
                for mb in range(2):
                    PE(pd.ap, ones_b.ap, PTx[mb].ap, mb == 0, mb == 1, [ones_b, PTx[mb]], [pd])
                ACT(rdx.ap, pd.ap, AF.Ln, [pd], [rdx])
                ACT(rdx.ap, rdx.ap, AF.Exp, [rdx], [rdx], scale=-1.0)
                for dc in range(2):
                    ps = psum()
                    for mb in range(2):
                        PE(ps.ap, Vm.ap[:, mb, (2 * hx + dc) * 128:(2 * hx + dc + 1) * 128], PTx[mb].ap, mb == 0, mb == 1, [Vm, PTx[mb]], [ps])
                    TT("dve", mixin.ap[:, 2 * hx + dc, :], ps.ap, rdx.ap, ALU.mult, [ps, rdx], [mixin.bufs[2 * hx + dc]])
            for n in range(4):
                for hf in range(2):
                    ps = psum()
                    for kc in range(8):
                        PE(ps.ap, mixin.ap[:, kc, n * 128:(n + 1) * 128], Wxo.ap[:, kc, hf * 512:(hf + 1) * 512], kc == 0, kc == 7, [mixin, Wxo], [ps])
                    STT("dve", xs.ap[:, n, hf * 512:(hf + 1) * 512], xs.ap[:, n, hf * 512:(hf + 1) * 512], DN_ALPHA, ps.ap, ALU.mult, ALU.add, [xs.bufs[n], ps], [xs.bufs[n]])
            ln4(xs, lnp["2"][0], lnp["2"][1], None)
            DMA("sp", H2[c0:c0 + 512, :].rearrange("(n p) d -> p n d", p=128), xs.ap, [xs], [], "stH2")

        ld_c_small(0)
        ld_c_x(0)
        chunk_loop(0, range(8))
        for t in range(NT):
            wo_ln1(t)
            if t + 1 < NT:
                ld_c_small(t + 1)
                chunk_loop(t + 1, range(0, 2))
            cps = ln4(xs, lnp["1"][0], lnp["1"][1], hb, defer=True)
            if t + 1 < NT:
                chunk_loop(t + 1, range(2, 5))
            cps()
            xattn(t)
            if t + 1 < NT:
                ld_c_x(t + 1)
                chunk_loop(t + 1, range(5, 8))
        S_.barrier()

    if "D" in phases:
        A.top = base00
        Wup = A.tile([128, 8, 4096], BF16, "Wup")
        Wdn = A.tile([128, 32, 1024], BF16, "Wdn")
        g3 = A.tile([128, 1024], F32, "g3"); b3 = A.tile([128, 1024], F32, "b3")
        load_bc(g3, ln3_g); load_bc(b3, ln3_b)
        for q in range(4):
            wload(Wup.ap[:, :, q * 1024:(q + 1) * 1024], w_up.rearrange("(kc p) n -> p kc n", p=128)[:, :, q * 1024:(q + 1) * 1024], [Wup], "wup")
            wload(Wdn.ap[:, q * 8:(q + 1) * 8, :], w_down.rearrange("(kc p) n -> p kc n", p=128)[:, q * 8:(q + 1) * 8, :], [Wdn], "wdn")
        ln_alloc(1)
        xs = MT(A.tile([128, 4, 1024], F32, "xs"))
        hb = MT(A.tile([128, 4, 1024], BF16, "hb"))
        hT = MT(A.tile([128, 8, 512], BF16, "hT"), 8)
        xs2 = MT(A.tile([128, 4, 1024], F32, "xs2"))
        xss = [xs, xs2]
        hids = [MT(A.tile([128, 8, 512], BF16, "hid"), 8) for _ in range(2)]
        tmpr = [A.tile([128, 512], F32, "tmpr") for _ in range(2)]
        ob = Buf("out")

        def ld_h2(t):
            DMA("sp", xss[t % 2].ap, H2[t * 512:(t + 1) * 512, :].rearrange("(n p) d -> p n d", p=128), [], [xss[t % 2]], f"xD{t % 2}")

        ld_h2(0)
        pctr = 0

        def conv_tr(t):
            xq_ = xss[t % 2]
            for n in range(4):
                CP("act" if n % 2 else "dve", hb.ap[:, n, :], xq_.ap[:, n, :], [xq_.bufs[n]], [hb.bufs[n]])
            transpose_blocks(hb, hT)

        def up_pass(ps_, hid):
            for hl in range(8):
                hc = ps_ * 8 + hl
                ps = psum()
                for kc in range(8):
                    PE(ps.ap, Wup.ap[:, kc, hc * 128:(hc + 1) * 128], hT.ap[:, kc, :], kc == 0, kc == 7, [Wup, hT.bufs[kc]], [ps])
                if hl % 2 == 0:
                    ACT(tmpr[0].ap, ps.ap, AF.Relu, [ps], [tmpr[0]])
                    TT("dve", hid.ap[:, hl, :], tmpr[0].ap, tmpr[0].ap, ALU.mult, [tmpr[0]], [hid.bufs[hl]])
                else:
                    TS("dve", tmpr[1].ap, ps.ap, 0.0, None, ALU.max, None, [ps], [tmpr[1]])
                    ACT(hid.ap[:, hl, :], tmpr[1].ap, AF.Square, [tmpr[1]], [hid.bufs[hl]])

        def down_pass(ps_, hid, xs):
            for n in range(4):
                for hf in range(2):
                    ps = psum()
                    for hl in range(8):
                        PE(ps.ap, hid.ap[:, hl, n * 128:(n + 1) * 128], Wdn.ap[:, ps_ * 8 + hl, hf * 512:(hf + 1) * 512], hl == 0, hl == 7, [hid.bufs[hl], Wdn], [ps])
                    xv = xs.ap[:, n, hf * 512:(hf + 1) * 512]
                    STT("dve", xv, xv, DN_ALPHA if ps_ == 0 else 1.0, ps.ap, ALU.mult, ALU.add, [xs.bufs[n], ps], [xs.bufs[n]])

        conv_tr(0)
        for t in range(NT):
            c0 = t * 512
            xs = xss[t % 2]
            if t + 1 < NT:
                ld_h2(t + 1)
            up_pass(0, hids[0])
            for ps_ in range(4):
                if ps_ + 1 < 4:
                    up_pass(ps_ + 1, hids[(ps_ + 1) % 2])
                down_pass(ps_, hids[ps_ % 2], xs)
            if t + 1 < NT:
                conv_tr(t + 1)
            ln4(xs, g3, b3, None)
            DMA("sp", out[c0:c0 + 512, :].rearrange("(n p) d -> p n d", p=128), xs.ap, [xs], [ob], "stOut")
        final_bufs.append(ob)

    keys = S_.finalize(final_waits=final_bufs)
    sems = {k: es.enter_context(nc.semaphore(f"s{i}")) for i, k in enumerate(keys)}
    with nc.Block() as block:
        S_.emit(block, sems)
    es.close()
    return nc


def _consts():
    ident = np.eye(128, dtype=np.float32)
    idx = np.arange(128)
    cm = (idx[None, :] // 16 >= idx[:, None] // 16).astype(np.float32)
    tri = (idx[None, :] >= idx[:, None]).astype(np.float32)
    rope = np.zeros((128, 3), np.float32)
    rope[:, 2] = np.float32(math.pi / 2)
    inv = (10000.0 ** (-np.arange(0, 32, 2, dtype=np.float32) / np.float32(32))).astype(np.float32)
    for r in range(32):
        rope[64 + r, 0] = inv[r % 16]
        rope[64 + r, 1] = -1.0 if r < 16 else 1.0
    kvec = np.tile(np.arange(-7, 9, dtype=np.float32)[None, :], (64, 1))
    cvec = np.tile(np.arange(128, dtype=np.float32)[None, :], (64, 1))
    return {"c_ident": ident, "c_cm": cm, "c_tri": tri, "c_rope": rope, "c_kvec": kvec, "c_cvec": cvec}


def _core_inputs(inp, b, S):
    f = lambda a: np.ascontiguousarray(a, dtype=np.float32)
    m = {
        "x": f(inp["x"][b, :S]), "mem": f(inp["mem"][b]),
        "positions": np.ascontiguousarray(inp["positions"][b, :S].reshape(1, S).astype(np.int32)),
        "ln_in_g": f(inp["ln_in_g"].reshape(1, 1024)), "ln_in_b": f(inp["ln_in_b"].reshape(1, 1024)),
        "w_in": f(inp["w_in"][0]),
        "s5_lam_re": f(inp["s5_lam_re"][0]), "s5_lam_im": f(inp["s5_lam_im"][0]), "s5_log_dt": f(inp["s5_log_dt"].reshape(1, 16)),
        "s5_b_re": f(inp["s5_b_re"][0]), "s5_b_im": f(inp["s5_b_im"][0]),
        "s5_c_re": f(inp["s5_c_re"][0]), "s5_c_im": f(inp["s5_c_im"][0]),
        "s5_d": f(inp["s5_d"].reshape(16, 16)),
        "w_glu": f(inp["w_glu"][0]), "q_norm_g": f(inp["q_norm_g"].reshape(256, 1)), "w_uq": f(inp["w_uq"][0]),
        "kv_norm_g": f(inp["kv_norm_g"].reshape(256, 1)), "w_ukv": f(inp["w_ukv"][0]),
        "w_oa": f(inp["w_oa"][0]), "w_o": f(inp["w_o"][0]),
        "ln1_g": f(inp["ln1_g"].reshape(1, 1024)), "ln1_b": f(inp["ln1_b"].reshape(1, 1024)),
        "w_xq": f(inp["w_xq"][0]), "w_xk": f(inp["w_xk"][0]), "w_xv": f(inp["w_xv"][0]), "w_xo": f(inp["w_xo"][0]),
        "ln2_g": f(inp["ln2_g"].reshape(1, 1024)), "ln2_b": f(inp["ln2_b"].reshape(1, 1024)),
        "w_up": f(inp["w_up"][0]), "w_down": f(inp["w_down"][0]),
        "ln3_g": f(inp["ln3_g"].reshape(1, 1024)), "ln3_b": f(inp["ln3_b"].reshape(1, 1024)),
    }
    m.update(_consts())
    return m


def kernel(**inputs):
    S = inputs["x"].shape[1]
    B = inputs["x"].shape[0]
    nc = build(S)
    in_maps = [_core_inputs(inputs, b, S) for b in range(B)]
    res = run_bass_kernel_spmd(nc, in_maps, core_ids=list(range(B)))
    return np.stack([np.asarray(r["out"], dtype=np.float32) for r in res.results], axis=0)
```
